# Optimizing a Trainium2 kernel written in Bass

```python
import jax
import jax.numpy as jnp
from jax import lax
import numpy as np

D_MODEL = 1024
BATCH = 4
SEQ = 8192
DEPTH = 2

GRID_W = 64
CTX_LEN = 256
NORM_EPS = 1e-6
N_BRANCH = 3

ML_W = D_MODEL
ML_HEADS = 4
ML_HD = ML_W // ML_HEADS
ML_CHUNK = 64
ML_CONV = 3

RK_W = D_MODEL
RK_HD = 64
RK_HEADS = RK_W // RK_HD
RK_DECAY_LORA = D_MODEL // 16
RK_A_LORA = D_MODEL // 16
RK_V_LORA = D_MODEL // 32
RK_GN_EPS = 64e-5
RK_MU_W = 3 * RK_W + 2 * RK_DECAY_LORA + 2 * RK_A_LORA

HG_W = D_MODEL
HG_EXPAND = 128
HG_HEADS = HG_W // HG_EXPAND
HG_DV = HG_W // HG_HEADS
HG_CHUNK = 16

IN_LAYOUT = (
    ('m_q', ML_W), ('m_k', ML_W), ('m_v', ML_W), ('m_o', ML_W), ('m_z', ML_W), ('m_if', 4 * ML_HEADS),
    ('r_r', RK_W), ('r_k', RK_W), ('r_v', RK_W), ('r_z', RK_W),
    ('r_wd', 2 * RK_DECAY_LORA), ('r_ad', 2 * RK_A_LORA),
    ('h_q', HG_W), ('h_f', 2 * HG_W), ('h_i', HG_W), ('h_z', HG_W),
    ('gate', N_BRANCH * D_MODEL),
)
IN_COLS = 5 * ML_W + 4 * ML_HEADS + 4 * RK_W + 2 * RK_DECAY_LORA + 2 * RK_A_LORA + 5 * HG_W + N_BRANCH * D_MODEL

kernel_name = 'hybrid_mlstm_rwkv7_hgrn2_dit'


def _in_slices():
    out, off = {}, 0
    for name, width in IN_LAYOUT:
        out[name] = (off, off + width)
        off += width
    return out


def _proj(h, w_in, name):
    lo, hi = _in_slices()[name]
    return h @ w_in[:, lo:hi]


def _rms_norm(x, w):
    xf = x.astype(jnp.float32)
    y = xf * lax.rsqrt(jnp.mean(xf * xf, axis=-1, keepdims=True) + NORM_EPS)
    return (y * w).astype(x.dtype)


def _dwconv_centered(u, w):
    k, ch = w.shape
    pad = k // 2
    return lax.conv_general_dilated(u, w[:, None, :].astype(u.dtype), window_strides=(1,),
                                    padding=((pad, pad),), dimension_numbers=('NWC', 'WIO', 'NWC'),
                                    feature_group_count=ch)


def _q_shift_grid(u):
    b, l, ch = u.shape
    rows = l // GRID_W
    g = jnp.pad(u.reshape(b, rows, GRID_W, ch), ((0, 0), (1, 1), (1, 1), (0, 0)))
    q4 = ch // 4
    parts = (g[:, 1:-1, :-2, :q4], g[:, 1:-1, 2:, q4:2 * q4],
             g[:, :-2, 1:-1, 2 * q4:3 * q4], g[:, 2:, 1:-1, 3 * q4:])
    return jnp.concatenate(parts, axis=-1).reshape(b, l, ch)


def _bi_shift_seq(u):
    half = u.shape[-1] // 2
    g = jnp.pad(u, ((0, 0), (1, 1), (0, 0)))
    return jnp.concatenate((g[:, :-2, :half], g[:, 2:, half:]), axis=-1)


def _token_shift(u, on_grid):
    return _q_shift_grid(u) if on_grid else _bi_shift_seq(u)


def _mlstm_chunk_scan(q, k, v, log_i, log_f, state):
    b_, h_, l, _ = q.shape
    t = ML_CHUNK
    nc = l // t

    def chunk(a):
        return jnp.moveaxis(a.reshape((b_, h_, nc, t) + a.shape[3:]), 2, 0)

    tri = jnp.tril(jnp.ones((t, t), dtype=bool))

    def step(carry, xs):
        c_mem, n_mem, m_prev = carry
        qc, kc, vc, ic, fc = xs
        bcum = jnp.cumsum(fc, axis=-1)
        dmat = jnp.where(tri, bcum[..., :, None] - bcum[..., None, :] + ic[..., None, :], -jnp.inf)
        inter = bcum + m_prev[..., None]
        m_t = jnp.maximum(jnp.max(dmat, axis=-1), inter)
        s = jnp.einsum('bhtd,bhsd->bhts', qc, kc) * jnp.exp(dmat - m_t[..., None])
        w_inter = jnp.exp(inter - m_t)
        num = jnp.einsum('bhts,bhse->bhte', s, vc) + w_inter[..., None] * jnp.einsum('bhtd,bhde->bhte', qc, c_mem)
        den = jnp.sum(s, axis=-1) + w_inter * jnp.einsum('bhtd,bhd->bht', qc, n_mem)
        h_out = num / jnp.maximum(jnp.abs(den), jnp.exp(-m_t))[..., None]
        g = bcum[..., -1:] - bcum + ic
        m_new = jnp.maximum(bcum[..., -1] + m_prev, jnp.max(g, axis=-1))
        wk = jnp.exp(g - m_new[..., None])
        dec = jnp.exp(bcum[..., -1] + m_prev - m_new)
        c_mem = dec[..., None, None] * c_mem + jnp.einsum('bhs,bhsd,bhse->bhde', wk, kc, vc)
        n_mem = dec[..., None] * n_mem + jnp.einsum('bhs,bhsd->bhd', wk, kc)
        return (c_mem, n_mem, m_new), h_out

    state, hs = lax.scan(step, state, (chunk(q), chunk(k), chunk(v), chunk(log_i), chunk(log_f)))
    return jnp.moveaxis(hs, 0, 2).reshape(b_, h_, l, -1), state


def _rwkv7_scan(r, w, k, v, a, b, state):
    def step(s_mem, xs):
        r_t, w_t, k_t, v_t, a_t, b_t = xs
        s_mem = (s_mem * w_t[:, :, None, :]
                 + jnp.einsum('bhij,bhj->bhi', s_mem, a_t)[..., None] * b_t[:, :, None, :]
                 + v_t[..., None] * k_t[:, :, None, :])
        return s_mem, jnp.einsum('bhij,bhj->bhi', s_mem, r_t)

    xs = tuple(jnp.moveaxis(a_, 1, 0) for a_ in (r, w, k, v, a, b))
    state, ys = lax.scan(step, state, xs)
    return jnp.moveaxis(ys, 0, 1), state


def _gla_chunk_scan(q, k, v, log_f, state):
    b_, h_, l, _ = q.shape
    t = HG_CHUNK
    nc = l // t

    def chunk(a):
        return jnp.moveaxis(a.reshape(b_, h_, nc, t, a.shape[-1]), 2, 0)

    tri = jnp.tril(jnp.ones((t, t), dtype=bool))

    def step(s_mem, xs):
        qc, kc, vc, fc = xs
        g = jnp.cumsum(fc, axis=-2)
        g_mid = g[..., t // 2 - 1:t // 2, :]
        att = jnp.einsum('bhtd,bhsd->bhts', qc * jnp.exp(g - g_mid), kc * jnp.exp(g_mid - g))
        att = jnp.where(tri, att, 0.0)
        o = jnp.einsum('bhts,bhse->bhte', att, vc) + jnp.einsum('bhtd,bhde->bhte', qc * jnp.exp(g), s_mem)
        g_last = g[..., -1:, :]
        s_mem = (jnp.exp(g_last[..., 0, :])[..., None] * s_mem
                 + jnp.einsum('bhsd,bhse->bhde', kc * jnp.exp(g_last - g), vc))
        return s_mem, o

    state, os_ = lax.scan(step, state, (chunk(q), chunk(k), chunk(v), chunk(log_f)))
    return jnp.moveaxis(os_, 0, 2).reshape(b_, h_, l, -1), state


def _mlstm_branch(h, p, init, need_out):
    b_, l, _ = h.shape
    nh, dh = ML_HEADS, ML_HD
    f32 = jnp.float32

    def heads(a):
        return a.astype(f32).reshape(b_, l, nh, dh).transpose(0, 2, 1, 3)

    q = heads(jax.nn.silu(_dwconv_centered(_proj(h, p['w_in'], 'm_q'), p['ml_conv'][0]))) * (dh ** -0.5)
    k = heads(jax.nn.silu(_dwconv_centered(_proj(h, p['w_in'], 'm_k'), p['ml_conv'][1])))
    v = heads(_proj(h, p['w_in'], 'm_v'))
    gl = (_proj(h, p['w_in'], 'm_if').reshape(b_, l, 2, 2, nh) + p['ml_if_b']).astype(f32)
    log_i = gl[:, :, :, 0].transpose(2, 0, 3, 1)
    log_f = jax.nn.log_sigmoid(gl[:, :, :, 1]).transpose(2, 0, 3, 1)
    h_f, st_f = _mlstm_chunk_scan(q, k, v, log_i[0], log_f[0], init[0])
    h_b, st_b = _mlstm_chunk_scan(jnp.flip(q, 2), jnp.flip(k, 2), jnp.flip(v, 2),
                                  jnp.flip(log_i[1], -1), jnp.flip(log_f[1], -1), init[1])
    if not need_out:
        return None, (st_f, st_b)
    y = h_f + jnp.flip(h_b, 2)
    y = y - jnp.mean(y, axis=-1, keepdims=True)
    y = y * lax.rsqrt(jnp.mean(y * y, axis=-1, keepdims=True) + NORM_EPS)
    y = y.transpose(0, 2, 1, 3).reshape(b_, l, ML_W) * p['ml_norm_w']
    o = jax.nn.sigmoid(_proj(h, p['w_in'], 'm_o').astype(f32))
    z = _proj(h, p['w_in'], 'm_z').astype(f32)
    return o * y * jax.nn.silu(z), (st_f, st_b)


def _rwkv7_branch(h, p, init, on_grid, v_first, need_out):
    b_, l, _ = h.shape
    nh, n, rd, ra = RK_HEADS, RK_HD, RK_DECAY_LORA, RK_A_LORA
    f32 = jnp.float32
    mu = p['rk_mu']

    def lerp_shift(u, m):
        return u + m * (_token_shift(u, on_grid) - u)

    def heads(a):
        return a.astype(f32).reshape(b_, l, nh, n)

    r = lerp_shift(_proj(h, p['w_in'], 'r_r'), mu[0:RK_W])
    k = lerp_shift(_proj(h, p['w_in'], 'r_k'), mu[RK_W:2 * RK_W])
    v = lerp_shift(_proj(h, p['w_in'], 'r_v'), mu[2 * RK_W:3 * RK_W])
    wd = _proj(h, p['w_in'], 'r_wd')
    ad = _proj(h, p['w_in'], 'r_ad')
    ow, oa = 3 * RK_W, 3 * RK_W + 2 * rd
    if p['rk_v0'] is None:
        v_first = v
    else:
        v = v + (v_first - v) * jax.nn.sigmoid(p['rk_v0'] + (h @ p['rk_v1']) @ p['rk_v2'])
    rh, kh, vh = heads(r), heads(k), heads(v)
    kk = heads(k * p['rk_kk'])
    kk = kk / jnp.maximum(jnp.sqrt(jnp.sum(kk * kk, axis=-1, keepdims=True)), 1e-12)
    k_a = p['rk_ka'].astype(f32).reshape(nh, n)

    def direction(d):
        xw = lerp_shift(wd[..., d * rd:(d + 1) * rd], mu[ow + d * rd:ow + (d + 1) * rd])
        xa = lerp_shift(ad[..., d * ra:(d + 1) * ra], mu[oa + d * ra:oa + (d + 1) * ra])
        log_w = -jax.nn.softplus(-(p['rk_w0'][d] + jnp.tanh(xw) @ p['rk_w2'][d])) - 0.5
        decay = jnp.exp(-jnp.exp(heads(log_w)))
        a = jax.nn.sigmoid(heads(p['rk_a0'][d] + xa @ p['rk_a2'][d]))
        kd = kh * (1.0 + (a - 1.0) * k_a)
        seq = (rh, decay, kd, vh, -kk, kk * a)
        if d == 1:
            seq = tuple(jnp.flip(a_, 1) for a_ in seq)
        y, s = _rwkv7_scan(*seq, init[d])
        if d == 1:
            y = jnp.flip(y, 1)
        return y, s, kd

    y_f, s_f, k_f = direction(0)
    y_b, s_b, k_b = direction(1)
    if not need_out:
        return None, (s_f, s_b), v_first
    r_k = p['rk_rk'].astype(f32).reshape(nh, n)
    bonus = jnp.sum(rh * (k_f + k_b) * r_k, axis=-1, keepdims=True) * vh
    y = y_f + y_b
    y = y - jnp.mean(y, axis=-1, keepdims=True)
    y = y * lax.rsqrt(jnp.mean(y * y, axis=-1, keepdims=True) + RK_GN_EPS)
    y = y.reshape(b_, l, RK_W) * p['rk_ln_w'] + p['rk_ln_b'] + bonus.reshape(b_, l, RK_W)
    z = _proj(h, p['w_in'], 'r_z').astype(f32)
    return y * jax.nn.silu(z), (s_f, s_b), v_first


def _hgrn2_branch(h, p, init, need_out):
    b_, l, _ = h.shape
    nh, dk, dv = HG_HEADS, HG_EXPAND, HG_DV
    f32 = jnp.float32
    q = jax.nn.silu(_proj(h, p['w_in'], 'h_q').astype(f32)).reshape(b_, l, nh, dk).transpose(0, 2, 1, 3)
    i_in = _proj(h, p['w_in'], 'h_i').astype(f32).reshape(b_, l, nh, dv).transpose(0, 2, 1, 3)
    fz = (_proj(h, p['w_in'], 'h_f').reshape(b_, l, 2, HG_W) + p['hg_f_b']).astype(f32)
    f = p['hg_lb'] + (1.0 - p['hg_lb']) * jax.nn.sigmoid(fz)
    f = f.reshape(b_, l, 2, nh, dk).transpose(2, 0, 3, 1, 4)
    log_f = jnp.log(f)
    k_in = 1.0 - f
    o_f, s_f = _gla_chunk_scan(q, k_in[0], i_in, log_f[0], init[0])
    o_b, s_b = _gla_chunk_scan(jnp.flip(q, 2), jnp.flip(k_in[1], 2), jnp.flip(i_in, 2),
                               jnp.flip(log_f[1], 2), init[1])
    if not need_out:
        return None, (s_f, s_b)
    o = o_f + jnp.flip(o_b, 2)
    o = o * lax.rsqrt(jnp.mean(o * o, axis=-1, keepdims=True) + NORM_EPS)
    o = o.transpose(0, 2, 1, 3).reshape(b_, l, HG_W) * p['hg_norm_w']
    z = _proj(h, p['w_in'], 'h_z').astype(f32)
    return o * jax.nn.silu(z), (s_f, s_b)


def _mixer(h, p, init, on_grid, v_first, need_out):
    u_m, st_m = _mlstm_branch(h, p, init[0], need_out)
    u_r, st_r, v_first = _rwkv7_branch(h, p, init[1], on_grid, v_first, need_out)
    u_h, st_h = _hgrn2_branch(h, p, init[2], need_out)
    states = (st_m, st_r, st_h)
    if not need_out:
        return None, states, v_first
    b_, l, _ = h.shape
    g = jax.nn.sigmoid((_proj(h, p['w_in'], 'gate').reshape(b_, l, N_BRANCH, D_MODEL) + p['gate_b']).astype(jnp.float32))
    merged = (g[:, :, 0] * (u_m @ p['w_pm']) + g[:, :, 1] * (u_r @ p['w_pr']) + g[:, :, 2] * (u_h @ p['w_ph']))
    return merged @ p['w_out'], states, v_first


def _zero_states(batch):
    f32 = jnp.float32
    ml = (jnp.zeros((batch, ML_HEADS, ML_HD, ML_HD), f32), jnp.zeros((batch, ML_HEADS, ML_HD), f32),
          jnp.zeros((batch, ML_HEADS), f32))
    rk = jnp.zeros((batch, RK_HEADS, RK_HD, RK_HD), f32)
    hg = jnp.zeros((batch, HG_HEADS, HG_EXPAND, HG_DV), f32)
    return ((ml, ml), (rk, rk), (hg, hg))


def setup_inputs(seed: int = 0) -> dict:
    key = jax.random.key(seed)
    ks = jax.random.split(key, 40)
    f32 = jnp.float32

    def nrm(i, shape, scale=1.0):
        return jax.random.normal(ks[i], shape, f32) * scale

    d, nv = DEPTH, DEPTH - 1
    return {
        'x': nrm(0, (BATCH, SEQ, D_MODEL)),
        'c': nrm(1, (BATCH, D_MODEL)),
        'ctx': nrm(2, (BATCH, CTX_LEN, D_MODEL)),
        'c_ctx': nrm(3, (D_MODEL,)),
        'norm_w': 1.0 + nrm(4, (d, D_MODEL), 0.02),
        'ada_w': nrm(5, (d, D_MODEL, 3 * D_MODEL), 0.3 * D_MODEL ** -0.5),
        'ada_b': nrm(6, (d, 3 * D_MODEL), 0.02),
        'w_in': nrm(7, (d, D_MODEL, IN_COLS), D_MODEL ** -0.5),
        'gate_b': nrm(8, (d, N_BRANCH, D_MODEL), 0.1),
        'ml_conv': nrm(9, (d, 2, ML_CONV, ML_W), ML_CONV ** -0.5),
        'ml_if_b': jnp.concatenate([nrm(10, (d, 2, 1, ML_HEADS), 0.1),
                                    jnp.linspace(3.0, 6.0, ML_HEADS, dtype=f32) + nrm(11, (d, 2, 1, ML_HEADS), 0.1)], axis=2),
        'ml_norm_w': 1.0 + nrm(12, (d, ML_W), 0.02),
        'rk_mu': jax.random.uniform(ks[13], (d, RK_MU_W), f32),
        'rk_w0': jnp.linspace(-6.0, 1.0, RK_W, dtype=f32) + nrm(14, (d, 2, RK_W), 0.1),
        'rk_w2': nrm(15, (d, 2, RK_DECAY_LORA, RK_W), 0.1 * RK_DECAY_LORA ** -0.5),
        'rk_a0': nrm(16, (d, 2, RK_W), 0.1),
        'rk_a2': nrm(17, (d, 2, RK_A_LORA, RK_W), 0.1 * RK_A_LORA ** -0.5),
        'rk_kk': 0.85 + nrm(18, (d, RK_W), 0.05),
        'rk_ka': 1.0 + nrm(19, (d, RK_W), 0.05),
        'rk_rk': nrm(20, (d, RK_W), 0.1),
        'rk_v0': 1.0 + nrm(21, (nv, RK_W), 0.1),
        'rk_v1': nrm(22, (nv, D_MODEL, RK_V_LORA), D_MODEL ** -0.5),
        'rk_v2': nrm(23, (nv, RK_V_LORA, RK_W), 0.1 * RK_V_LORA ** -0.5),
        'rk_ln_w': 1.0 + nrm(24, (d, RK_W), 0.02),
        'rk_ln_b': nrm(25, (d, RK_W), 0.02),
        'hg_f_b': 2.0 + nrm(26, (d, 2, HG_W), 0.1),
        'hg_lb': 1.0 + nrm(27, (2, d, HG_W), 0.1),
        'hg_norm_w': 1.0 + nrm(28, (d, HG_W), 0.02),
        'w_pm': nrm(29, (d, ML_W, D_MODEL), ML_W ** -0.5),
        'w_pr': nrm(30, (d, RK_W, D_MODEL), RK_W ** -0.5),
        'w_ph': nrm(31, (d, HG_W, D_MODEL), HG_W ** -0.5),
        'w_out': nrm(32, (d, D_MODEL, D_MODEL), D_MODEL ** -0.5),
        'final_norm_w': 1.0 + nrm(33, (D_MODEL,), 0.02),
    }


def reference(x, c, ctx, c_ctx, norm_w, ada_w, ada_b, w_in, gate_b, ml_conv, ml_if_b, ml_norm_w,
              rk_mu, rk_w0, rk_w2, rk_a0, rk_a2, rk_kk, rk_ka, rk_rk, rk_v0, rk_v1, rk_v2, rk_ln_w, rk_ln_b,
              hg_f_b, hg_lb, hg_norm_w, w_pm, w_pr, w_ph, w_out, final_norm_w):
    batch = x.shape[0]
    lb_p = jax.nn.softmax(hg_lb.astype(jnp.float32), axis=1)
    lower_bounds = jnp.cumsum(lb_p, axis=1) - lb_p[:, :1]
    xs, cs = x, ctx
    vf_x, vf_c = None, None
    for l in range(DEPTH):
        last = l == DEPTH - 1
        p = {'w_in': w_in[l], 'gate_b': gate_b[l], 'ml_conv': ml_conv[l], 'ml_if_b': ml_if_b[l],
             'ml_norm_w': ml_norm_w[l], 'rk_mu': rk_mu[l], 'rk_w0': rk_w0[l], 'rk_w2': rk_w2[l],
             'rk_a0': rk_a0[l], 'rk_a2': rk_a2[l], 'rk_kk': rk_kk[l], 'rk_ka': rk_ka[l], 'rk_rk': rk_rk[l],
             'rk_v0': rk_v0[l - 1] if l > 0 else None, 'rk_v1': rk_v1[l - 1] if l > 0 else None,
             'rk_v2': rk_v2[l - 1] if l > 0 else None, 'rk_ln_w': rk_ln_w[l], 'rk_ln_b': rk_ln_b[l],
             'hg_f_b': hg_f_b[l], 'hg_lb': lower_bounds[:, l], 'hg_norm_w': hg_norm_w[l],
             'w_pm': w_pm[l], 'w_pr': w_pr[l], 'w_ph': w_ph[l], 'w_out': w_out[l]}
        mod_x = jax.nn.silu(c) @ ada_w[l] + ada_b[l]
        mod_c = jax.nn.silu(c_ctx) @ ada_w[l] + ada_b[l]
        shift_x, scale_x, gate_x = jnp.split(mod_x[:, None, :], 3, axis=-1)
        shift_c, scale_c, gate_c = jnp.split(mod_c, 3, axis=-1)
        hc = _rms_norm(cs, norm_w[l]) * (1.0 + scale_c) + shift_c
        out_c, st_c, vf_c = _mixer(hc, p, _zero_states(batch), False, vf_c, not last)
        hx = _rms_norm(xs, norm_w[l]) * (1.0 + scale_x) + shift_x
        out_x, _, vf_x = _mixer(hx, p, st_c, True, vf_x, True)
        xs = (xs + gate_x * out_x).astype(x.dtype)
        if not last:
            cs = (cs + gate_c * out_c).astype(ctx.dtype)
    return _rms_norm(xs, final_norm_w)
```

```python
import os
import numpy as np
import ml_dtypes
import concourse.bass as bass
import concourse.mybir as mybir
from concourse.bass_utils import run_bass_kernel_spmd

F32 = mybir.dt.float32
BF16 = mybir.dt.bfloat16
AF = mybir.ActivationFunctionType
ALU = mybir.AluOpType
AX = mybir.AxisListType

D = 1024
NB = 4
GRID_W = 64
EPS = 1e-6
RK_GN_EPS = 64e-5
REF = 63
WDEC = 0.6065306597126334
STAGE = int(os.environ.get('KSTAGE', '99'))
SUB = int(os.environ.get('KSUB', '99'))

OFF = {}
_o = 0
for _n, _w in (('m_q', 1024), ('m_k', 1024), ('m_v', 1024), ('m_o', 1024), ('m_z', 1024), ('m_if', 16),
               ('r_r', 1024), ('r_k', 1024), ('r_v', 1024), ('r_z', 1024), ('r_wd', 128), ('r_ad', 128),
               ('h_q', 1024), ('h_f', 2048), ('h_i', 1024), ('h_z', 1024), ('gate', 3072)):
    OFF[_n] = _o
    _o += _w
IN_COLS = _o


class Buf:
    __slots__ = ("name", "t", "lw", "rd", "dsem", "dcnt")

    def __init__(self, name, t=None):
        self.name = name
        self.t = t
        self.lw = None
        self.rd = []
        self.dsem = None
        self.dcnt = 0

    def __getitem__(self, idx):
        return self.t[idx]


class FW:
    def __init__(self, nc):
        self.nc = nc
        self.eng = {"pe": nc.tensor, "act": nc.scalar, "dve": nc.vector, "pool": nc.gpsimd, "sp": nc.sync}
        self.esem = {}
        self.ecnt = {}
        self.waited = {k: {} for k in self.eng}
        self._ctx = []
        for k in ("pe", "act", "dve", "pool"):
            cm = nc.semaphore("s_" + k)
            self.esem[k] = cm.__enter__()
            self._ctx.append(cm)
            self.ecnt[k] = 0
        self.nins = {k: 0 for k in self.eng}
        self.log = []
        self.pend = {k: [] for k in self.eng}

    def sb(self, name, shape, dt=F32):
        cm = self.nc.sbuf_tensor(name, list(shape), dt)
        t = cm.__enter__()
        self._ctx.append(cm)
        return Buf(name, t)

    def ps(self, name, shape, dt=F32):
        cm = self.nc.psum_tensor(name, list(shape), dt)
        t = cm.__enter__()
        self._ctx.append(cm)
        return Buf(name, t)

    def dram(self, name, shape, dt=F32, kind="Internal"):
        t = self.nc.dram_tensor(name, list(shape), dt, kind=kind)
        return Buf(name, t)

    def _dsem(self, b):
        if b.dsem is None:
            cm = self.nc.semaphore("d_" + b.name)
            b.dsem = cm.__enter__()
            self._ctx.append(cm)
        return b.dsem

    def _need(self, e, reads, writes):
        evs = []
        for b in reads:
            if b.lw is not None:
                evs.append(b.lw)
            if b.name[0] == 'p' and b.name[1] in 'AHTX':
                evs.extend(ev for ev in b.rd if ev[0] is not self.esem.get(e))
        for b in writes:
            if b.lw is not None:
                evs.append(b.lw)
            evs.extend(b.rd)
        w = self.waited[e]
        best = {}
        for (s, v) in evs:
            k = id(s)
            if w.get(k, 0) >= v:
                continue
            if k not in best or best[k][1] < v:
                best[k] = (s, v)
        for k, (s, v) in best.items():
            w[k] = v
            if e == "pe" and s is self.esem["pe"]:
                continue
            self.eng[e].wait_ge(s, v)
            self.pend[e].append((id(s), v))

    def _done(self, ev, reads, writes):
        for b in writes:
            b.lw = ev
            b.rd = []
        for b in reads:
            if b in writes:
                continue
            b.rd.append(ev)
            if len(b.rd) > 16:
                m = {}
                for (s, v) in b.rd:
                    if id(s) not in m or m[id(s)][1] < v:
                        m[id(s)] = (s, v)
                b.rd = list(m.values())

    def op(self, e, fn, reads=(), writes=()):
        reads = [b for b in reads if b is not None]
        writes = [b for b in writes if b is not None]
        self._need(e, reads, writes)
        ins = fn(self.eng[e])
        self.ecnt[e] += 1
        ins.then_inc(self.esem[e], 1)
        self.log.append((e, self.pend[e], (id(self.esem[e]), 1), False))
        self.pend[e] = []
        self._done((self.esem[e], self.ecnt[e]), reads, writes)
        self.nins[e] += 1
        return ins

    def dma(self, q, out_ap, in_ap, reads=(), writes=(), sem_buf=None):
        reads = [b for b in reads if b is not None]
        writes = [b for b in writes if b is not None]
        sb = sem_buf if sem_buf is not None else (writes[0] if writes else reads[0])
        sem = self._dsem(sb)
        self._need(q, reads, writes)
        if sb.dcnt > 0 and self.waited[q].get(id(sem), 0) < sb.dcnt:
            self.eng[q].wait_ge(sem, sb.dcnt)
            self.waited[q][id(sem)] = sb.dcnt
            self.pend[q].append((id(sem), sb.dcnt))
        ins = self.eng[q].dma_start(out=out_ap, in_=in_ap)
        sb.dcnt += 16
        ins.then_inc(sem, 16)
        self.log.append((q, self.pend[q], (id(sem), 16), True))
        self.pend[q] = []
        self._done((sem, sb.dcnt), reads, writes)
        self.nins[q] += 1
        return ins

    def wait_all(self, e, bufs):
        self._need(e, bufs, [])

    def simulate(self):
        queues = {}
        for (q, waits, inc, isdma) in self.log:
            queues.setdefault(q, []).append((waits, inc))
        pos = {q: 0 for q in queues}
        sem = {}
        progress = True
        while progress:
            progress = False
            for q, lst in queues.items():
                while pos[q] < len(lst):
                    waits, inc = lst[pos[q]]
                    if all(sem.get(s, 0) >= v for s, v in waits):
                        sem[inc[0]] = sem.get(inc[0], 0) + inc[1]
                        pos[q] += 1
                        progress = True
                    else:
                        break
        stuck = {q: (pos[q], len(lst)) for q, lst in queues.items() if pos[q] < len(lst)}
        return stuck

    def close(self):
        for cm in reversed(self._ctx):
            cm.__exit__(None, None, None)
        self._ctx = []


def _const_f32():
    j = np.arange(128)[:, None]
    t = np.arange(128)[None, :]
    c = {}
    c['mI'] = (j <= t).astype(np.float32)
    c['mS'] = (j < t).astype(np.float32)
    c['mSL'] = (j > t).astype(np.float32)
    c['ones'] = np.ones((128, 128), np.float32)
    c['ident'] = np.eye(128, dtype=np.float32)
    inc = (j <= t).astype(np.float32) - (j <= REF).astype(np.float32)
    exc = (j < t).astype(np.float32) - (j <= REF).astype(np.float32)
    c['rk_inc'] = -WDEC * inc
    c['rk_exc'] = -WDEC * exc
    c['rk_sel2'] = -WDEC * np.broadcast_to((j > REF).astype(np.float32), (128, 128)).copy()
    sv = np.zeros((128, 128), np.float32)
    sv[:, 0] = -WDEC * (np.arange(128) <= REF)
    sv[:, 1] = -WDEC * (np.arange(128) > REF)
    c['rk_selv'] = sv
    c['hg_inc'] = inc
    hv = np.zeros((128, 128), np.float32)
    hv[:, 0] = (np.arange(128) <= REF)
    hv[:, 1] = (np.arange(128) > REF)
    c['hg_selv'] = hv
    return c


CF_NAMES = ['mI', 'mS', 'mSL', 'ones', 'ident', 'rk_inc', 'rk_exc', 'rk_sel2', 'rk_selv', 'hg_inc', 'hg_selv']


def _shift_mats(kind, first, last, d):
    tt = np.arange(128)
    def M(pairs):
        m = np.zeros((128, 128), np.float32)
        for s, t in pairs:
            m[s, t] = 1.0
        return m
    cP_C = M([(t - 1, t) for t in tt if t >= 1])
    cN_C = M([(t + 1, t) for t in tt if t <= 126])
    cP_H = M([] if first else [(63, 0)])
    cN_H = M([] if last else [(64, 127)])
    Z = np.zeros((128, 128), np.float32)
    if kind == 'x':
        m1 = (M([(t - 1, t) for t in tt if t % 64 != 0]), Z)
        p1 = (M([(t + 1, t) for t in tt if t % 64 != 63]), Z)
        m64 = (M([(t - 64, t) for t in tt if t >= 64]), M([] if first else [(t, t) for t in tt if t < 64]))
        p64 = (M([(t + 64, t) for t in tt if t < 64]), M([] if last else [(t, t) for t in tt if t >= 64]))
        qs = [m1, p1, m64, p64] if d == 0 else [p1, m1, p64, m64]
    else:
        P = (cP_C, cP_H)
        N = (cN_C, cN_H)
        qs = [P, P, N, N] if d == 0 else [N, N, P, P]
    out = []
    for q in qs:
        out += [q[0], q[1]]
    out += [cP_C, cP_H, cN_C, cN_H]
    return np.stack(out)


def _wdad_perm():
    idx = []
    for q in range(4):
        for sub in range(3):
            idx += [sub * 64 + q * 16 + i for i in range(16)]
    return np.array(idx)


def build_scan(tiles, layer):
    NT = len(tiles)
    nvar = max(v for _, v in tiles) + 1
    nc = bass.Bass("TRN2", target_bir_lowering=False)
    fw = FW(nc)
    op = fw.op
    L1 = layer > 0

    def din(name, shape, dt=F32):
        return fw.dram(name, shape, dt, kind="ExternalInput")

    def dout(name, shape, dt=F32):
        return fw.dram(name, shape, dt, kind="ExternalOutput")

    xseq = din("xseq", [(NT + 1) * 128, D])
    WS = din("WS", [D, 5312])
    NWU = 4104 + (32 if L1 else 0)
    WU = din("WU", [D, NWU])
    cfm = din("cfm", [128, 8, 2])
    adaw = din("adaw", [D, 3072])
    adab = din("adab", [128, 24])
    normw = din("normw", [128, 8])
    CF = din("CF", [len(CF_NAMES), 128, 128])
    SH = din("SH", [nvar * 12, 128, 128])
    bcin = din("bcin", [1, 3264 + 3072 + 2048])
    convin = din("convin", [1, 4 * 1536])
    NBIAS = 8 + 1024 + 1024 + 2048 + (1024 if L1 else 0)
    biasin = din("biasin", [1, NBIAS])
    W2 = din("W2", [192, 3, 1024])
    if L1:
        V2 = din("V2", [32, 1024])
        vfin = din("vfin", [NT, 128, D])
    st_ml_in = din("st_ml_in", [128, 4, 2, 257])
    st_rk_in = din("st_rk_in", [64, 16, 64])
    st_hg_in = din("st_hg_in", [128, 8, 128])
    y_m = dout("y_m", [NT, 128, D])
    y_r = dout("y_r", [NT, 128, D])
    y_h = dout("y_h", [NT, 128, D])
    bon_o = dout("bon_o", [NT, 128, D])
    if not L1:
        vf_o = dout("vf_o", [NT, 128, D])
    st_ml_out = dout("st_ml_out", [128, 4, 2, 257])
    st_rk_out = dout("st_rk_out", [64, 16, 64])
    st_hg_out = dout("st_hg_out", [128, 8, 128])

    cf = fw.sb("cf", [128, len(CF_NAMES), 128], F32)
    cfi = {n: i for i, n in enumerate(CF_NAMES)}
    CFm = lambda n: cf[:, cfi[n], :]
    sh = fw.sb("sh", [128, nvar * 12, 128], BF16)
    identb = fw.sb("identb", [128, 128], BF16)
    mask4 = fw.sb("mask4", [128, 512], F32)
    bc = fw.sb("bc", [128, 3264 + 2048 + 1024], F32)
    MU0, KK0, LB0 = 0, 3264, 3264 + 2048
    cvb = fw.sb("cvb", [128, 1536], F32)
    bias = fw.sb("bias", [33, NBIAS], BF16)
    ones33 = fw.sb("ones33", [33, 128], BF16)
    w2s = fw.sb("w2s", [128, 2, 3, 1024], BF16)
    if L1:
        v2s = fw.sb("v2s", [32, 1024], BF16)
    modA = fw.sb("modA", [128, 8, 2], F32)
    modB = fw.sb("modB", [128, 8, 2], F32)
    wbuf = [fw.sb("wbuf%d" % i, [128, 8, 512], BF16) for i in range(2)]
    hT = [fw.sb("hT%d" % i, [128, 8, 128], BF16) for i in range(2)]
    small = fw.sb("small", [128, 64], F32)
    Fs = [fw.sb("F%d" % i, [128, D], F32) for i in range(10)]
    Hs = [fw.sb("H%d" % i, [128, D], BF16) for i in range(12)]
    st_ml = fw.sb("st_ml", [128, 4, 2, 257], F32)
    st_mlb = fw.sb("st_mlb", [128, 4, 2, 257], BF16)
    st_rk = fw.sb("st_rk", [64, 16, 64], F32)
    st_rkb = fw.sb("st_rkb", [64, 16, 64], BF16)
    st_hg = fw.sb("st_hg", [128, 8, 128], F32)
    vp = fw.sb("vp", [128, 4, 258], BF16)
    art = fw.sb("art", [128, 8, 256], BF16)
    btT = fw.sb("btT", [128, 8, 128], BF16)
    ktT = fw.sb("ktT", [128, 8, 128], BF16)
    nbab = fw.sb("nbab", [128, 512], BF16)
    pq = [fw.sb("pq%d" % i, [128, 128], BF16) for i in range(2)]
    xq = [fw.sb("xq%d" % i, [128, 128], BF16) for i in range(2)]
    rf = fw.sb("rf", [128, 128], F32)
    rb = fw.sb("rb", [128, 128], BF16)
    m2b = fw.sb("m2b", [128, 64], BF16)
    m2T = fw.sb("m2T", [64, 128], BF16)
    Lb = fw.sb("Lb", [64, 64], BF16)
    dcp = fw.sb("dcp", [64, 64], F32)
    ycs = fw.sb("ycs", [128, 64], F32)
    e12 = fw.sb("e12", [64, 16, 4], F32)
    i64e = fw.sb("i64e", [64, 64], F32)
    sml = fw.sb("sml", [128, 64], F32)
    lhm = fw.sb("lhm", [128, 128], F32)
    dmat = fw.sb("dmat", [128, 128], F32)
    pmat = fw.sb("pmat", [128, 128], BF16)
    asb = fw.sb("asb", [128, 257], F32)
    num = fw.sb("num", [128, 257], F32)
    ktl = fw.sb("ktl", [128, 256], BF16)
    hge = fw.sb("hge", [128, 8, 2], F32)
    hgeg = fw.sb("hgeg", [128, 128], F32)
    hgen = fw.sb("hgen", [128, 128], F32)
    qtl = fw.sb("qtl", [128, 128], BF16)
    kTl = fw.sb("kTl", [128, 128], BF16)
    attb = fw.sb("attb", [128, 128], BF16)
    s0t = fw.sb("s0t", [128, 128], BF16)
    s0f = fw.sb("s0f", [128, 128], F32)

    pA = [fw.ps("pA%d" % i, [128, 512], F32) for i in range(2)]
    pH = fw.ps("pH", [128, 512], F32)
    pT = fw.ps("pT", [128, 1024], BF16)
    pX = [fw.ps("pX%d" % i, [128, 512], F32) for i in range(4)]

    fw.dma("sp", cf[:], CF[:].rearrange("n p f -> p n f"), reads=[CF], writes=[cf])
    for c0 in range(0, nvar * 12, 8):
        n_ = min(8, nvar * 12 - c0)
        stg = Fs[(c0 // 8) % 4]
        fw.dma("sp", stg[:, 0:n_ * 128].rearrange("p (n f) -> p n f", n=n_), SH[c0:c0 + n_].rearrange("n p f -> p n f"), reads=[SH], writes=[stg])
        op("pool", lambda e: e.tensor_copy(out=sh[:, c0:c0 + n_, :], in_=stg[:, 0:n_ * 128].rearrange("p (n f) -> p n f", n=n_)), reads=[stg], writes=[sh])
    for j in range(3):
        stg = Fs[4 + j]
        fw.dma("sp", stg[:, :], W2[0:128, j, :], reads=[W2], writes=[stg])
        op("pool", lambda e: e.tensor_copy(out=w2s[:, 0, j, :], in_=stg[:, :]), reads=[stg], writes=[w2s])
    for j in range(3):
        stg = Fs[7 + j]
        fw.dma("sp", stg[0:64, :], W2[128:192, j, :], reads=[W2], writes=[stg])
        op("pool", lambda e: e.tensor_copy(out=w2s[0:64, 1, j, :], in_=stg[0:64, :]), reads=[stg], writes=[w2s])
    if L1:
        fw.dma("sp", Fs[3][0:32, :], V2[:], reads=[V2], writes=[Fs[3]])
        op("pool", lambda e: e.tensor_copy(out=v2s[:], in_=Fs[3][0:32, :]), reads=[Fs[3]], writes=[v2s])
    fw.dma("sp", st_ml[:], st_ml_in[:], reads=[st_ml_in], writes=[st_ml])
    fw.dma("sp", st_rk[:], st_rk_in[:], reads=[st_rk_in], writes=[st_rk])
    fw.dma("sp", st_hg[:], st_hg_in[:], reads=[st_hg_in], writes=[st_hg])
    op("dve", lambda e: e.tensor_copy(out=st_mlb[:], in_=st_ml[:]), reads=[st_ml], writes=[st_mlb])
    op("dve", lambda e: e.tensor_copy(out=st_rkb[:], in_=st_rk[:]), reads=[st_rk], writes=[st_rkb])
    op("dve", lambda e: e.tensor_copy(out=identb[:], in_=CFm('ident')), reads=[cf], writes=[identb])
    for k in range(4):
        nm = 'mS' if k % 2 == 0 else 'mI'
        op("dve", lambda e, k=k, nm=nm: e.tensor_copy(out=mask4[:, k * 128:(k + 1) * 128], in_=CFm(nm)), reads=[cf], writes=[mask4])
    op("pool", lambda e: e.memset(ones33[:], 1.0), writes=[ones33])
    op("pool", lambda e: e.memset(vp[:], 1.0), writes=[vp])
    op("pool", lambda e: e.memset(bias[:], 0.0), writes=[bias])
    fw.dma("sp", bc[0:1, 0:NBIAS], biasin[:], reads=[biasin], writes=[bc])
    fw.dma("sp", bc[32:33, 0:NBIAS], biasin[:], reads=[biasin], writes=[bc])
    op("dve", lambda e: e.tensor_copy(out=bias[0:1, :], in_=bc[0:1, 0:NBIAS]), reads=[bc], writes=[bias])
    op("dve", lambda e: e.tensor_copy(out=bias[32:33, :], in_=bc[32:33, 0:NBIAS]), reads=[bc], writes=[bias])
    op("dve", lambda e: e.tensor_tensor(out=bc[32:33, 0:NBIAS], in0=bc[32:33, 0:NBIAS], in1=bias[32:33, :], op=ALU.subtract),
       reads=[bc, bias], writes=[bc])
    op("dve", lambda e: e.tensor_copy(out=bias[32:33, :], in_=bc[32:33, 0:NBIAS]), reads=[bc], writes=[bias])
    BI_IF, BI_F, BI_W0, BI_A0, BI_V0 = 0, 8, 1032, 2056, 4104
    fw.dma("sp", bc[:, 0:5312], bcin[0, 0:5312].partition_broadcast(128), reads=[bcin], writes=[bc])
    if L1:
        fw.dma("sp", Fs[0][:, 0:1024], bcin[0, 6336:7360].partition_broadcast(128), reads=[bcin], writes=[Fs[0]])
        fw.dma("sp", Fs[1][:, 0:1024], bcin[0, 7360:8384].partition_broadcast(128), reads=[bcin], writes=[Fs[1]])
        op("dve", lambda e: e.tensor_tensor(out=Fs[0][:], in0=Fs[0][:], in1=Fs[1][:], op=ALU.subtract), reads=[Fs[0], Fs[1]], writes=[Fs[0]])
        op("act", lambda e: e.activation(out=bc[:, LB0:LB0 + 1024], in_=Fs[0][:], func=AF.Sigmoid), reads=[Fs[0]], writes=[bc])
    else:
        op("pool", lambda e: e.memset(bc[:, LB0:LB0 + 1024], 0.0), writes=[bc])

    if STAGE >= 1:
        emit_modulation(fw, nc, cfm, adaw, adab, normw, modA, modB, None, wbuf, pX[0], Fs[2], Hs[1], [Fs[3], Fs[4], Fs[5], Fs[6]])

    def load_norm_hT(ti, which, slot, sidx):
        x_ = Fs[7 + slot]
        r0 = 64 + ti * 128
        if which == 'C':
            fw.dma("sp", x_[:], xseq[r0:r0 + 128, :], reads=[xseq], writes=[x_])
        else:
            fw.dma("sp", x_[0:64, :], xseq[r0 - 64:r0, :], reads=[xseq], writes=[x_])
            fw.dma("sp", x_[64:128, :], xseq[r0 + 128:r0 + 192, :], reads=[xseq], writes=[x_])
        emit_norm_hT(fw, x_, hT[slot], modA, modB, sidx, small, Fs[9], CFm('ident'), cf, pX[3], pX[2])

    wcount = [0]
    groups = [(WS, i * 512, 512) for i in range(10)] + [(WS, 5120, 192)] + \
             [(WU, i * 512, 512) for i in range(8)] + [(WU, 4096, 8)] + ([(WU, 4104, 32)] if L1 else [])
    gidx = {(id(g[0]), g[1]): i for i, g in enumerate(groups)}
    wsb = fw.dram("wsb", [128, len(groups), 8, 512], BF16)
    prep_ev = []
    cnt_ = 0
    for gi, (Wd, c0, n) in enumerate(groups):
        for k0 in range(0, 8, 2):
            stg = Fs[cnt_ % 6]
            hb = Hs[cnt_ % 6]
            eng_ = ("pool", "dve", "act")[cnt_ % 3]
            cnt_ += 1
            sv = stg[:, 0:2 * n].rearrange("p (k n) -> p k n", k=2)
            hv = hb[:, 0:2 * n].rearrange("p (k n) -> p k n", k=2)
            fw.dma("sp", sv, Wd[k0 * 128:(k0 + 2) * 128, c0:c0 + n].rearrange("(k p) n -> p k n", p=128), reads=[Wd], writes=[stg])
            if eng_ == "act":
                op("act", lambda e: e.copy(out=hv, in_=sv), reads=[stg], writes=[hb])
            else:
                op(eng_, lambda e: e.tensor_copy(out=hv, in_=sv), reads=[stg], writes=[hb])
            fw.dma("sp", wsb[:, gi, k0:k0 + 2, 0:n], hv, reads=[hb], writes=[wsb], sem_buf=hb)
            prep_ev.append(wsb.lw)
    for (s_, v_) in prep_ev:
        if fw.waited["sp"].get(id(s_), 0) < v_:
            nc.sync.wait_ge(s_, v_)
            fw.waited["sp"][id(s_)] = v_
            fw.pend["sp"].append((id(s_), v_))

    def load_w(Wd, c0, n):
        wb = wbuf[wcount[0] % 2]
        wcount[0] += 1
        gi = gidx[(id(Wd), c0)]
        fw.dma("sp", wb[:, :, 0:n], wsb[:, gi, :, 0:n], reads=[wsb], writes=[wb])
        return wb

    def mm_proj(ps, n, hTt, wb, bias_off=None):
        for k in range(8):
            op("pe", lambda e, k=k: e.matmul(ps[:, 0:n], lhsT=hTt[:, k, :], rhs=wb[:, k, 0:n], start=(k == 0),
                                             stop=(k == 7 and bias_off is None)), reads=[hTt, wb], writes=[ps])
        if bias_off is not None:
            op("pe", lambda e: e.matmul(ps[:, 0:n], lhsT=ones33[:, :], rhs=bias[:, bias_off:bias_off + n], start=False, stop=True),
               reads=[ones33, bias], writes=[ps])

    def transpose_to(dst_ap, dst_buf, src_ap, src_buf, np_in, nf_in, col, evac="act", scale=None):
        op("pe", lambda e: e.transpose(out=pT[0:nf_in, col:col + np_in], in_=src_ap, identity=identb[0:np_in, 0:np_in]),
           reads=[src_buf, identb], writes=[pT])

    for ti, (kind, var) in enumerate(tiles):
        if STAGE < 2:
            continue
        sidx = 1 if kind == 'c' else 0
        shb = var * 12
        load_norm_hT(ti, 'C', 0, sidx)
        load_norm_hT(ti, 'H', 1, sidx)
        hC, hH = hT[0], hT[1]

        if STAGE < 3:
            continue
        qk = Hs[0]
        for g in range(4):
            wb = load_w(WS, g * 512, 512)
            if SUB < -2:
                continue
            mm_proj(pA[g % 2], 512, hC, wb)
            if SUB < -1:
                continue
            mm_proj(pH, 512, hH, wb)
            if SUB < 0:
                continue
            uC, uCb, uHb = Fs[0], Hs[2], Hs[3]
            op("act", lambda e: e.copy(out=uC[:, 0:512], in_=pA[g % 2][:, :]), reads=[pA[g % 2]], writes=[uC])
            op("dve", lambda e: e.tensor_copy(out=uCb[:, 0:512], in_=uC[:, 0:512]), reads=[uC], writes=[uCb])
            op("act", lambda e: e.copy(out=uHb[:, 0:512], in_=pH[:, :]), reads=[pH], writes=[uHb])
            if SUB < 1:
                continue
            acc = Fs[1]
            fw.dma("sp", cvb[:], convin[0, g * 1536:(g + 1) * 1536].partition_broadcast(128), reads=[convin], writes=[cvb])
            op("dve", lambda e: e.tensor_tensor(out=acc[:, 0:512], in0=uC[:, 0:512], in1=cvb[:, 512:1024], op=ALU.mult),
               reads=[uC, cvb], writes=[acc])
            if SUB < 2:
                continue
            for tap, (mc, mh) in ((0, (8, 9)), (2, (10, 11))):
                op("pe", lambda e: e.matmul(pH[:, :], lhsT=sh[:, shb + mc, :], rhs=uCb[:, 0:512], start=True, stop=False),
                   reads=[sh, uCb], writes=[pH])
                op("pe", lambda e: e.matmul(pH[:, :], lhsT=sh[:, shb + mh, :], rhs=uHb[:, 0:512], start=False, stop=True),
                   reads=[sh, uHb], writes=[pH])
                tmp = Fs[2]
                op("dve", lambda e, tap=tap: e.tensor_tensor(out=tmp[:, 0:512], in0=pH[:, :], in1=cvb[:, tap * 512:tap * 512 + 512], op=ALU.mult),
                   reads=[pH, cvb], writes=[tmp])
                op("dve", lambda e: e.tensor_tensor(out=acc[:, 0:512], in0=acc[:, 0:512], in1=tmp[:, 0:512], op=ALU.add),
                   reads=[acc, tmp], writes=[acc])
            if SUB < 3:
                continue
            dst = Hs[0] if g < 2 else Hs[1]
            op("act", lambda e: e.activation(out=dst[:, (g % 2) * 512:(g % 2) * 512 + 512], in_=acc[:, 0:512], func=AF.Silu),
               reads=[acc], writes=[dst])
        qtm, ktm = Hs[0], Hs[1]
        if SUB < 4:
            continue
        qTt, kTt = Hs[4], Hs[5]
        for (srcb, dstb, scl) in ((qtm, qTt, 0.0625), (ktm, kTt, 1.0)):
            for k in range(8):
                op("pe", lambda e, k=k: e.transpose(out=pT[:, k * 128:(k + 1) * 128], in_=srcb[:, k * 128:(k + 1) * 128], identity=identb[:]),
                   reads=[srcb, identb], writes=[pT])
            op("act", lambda e: e.activation(out=dstb[:, :], in_=pT[:, :], func=AF.Copy, scale=scl), reads=[pT], writes=[dstb])
        if STAGE < 4:
            continue
        wb = load_w(WU, 0, 512)
        mm_proj(pA[0], 512, hC, wb)
        op("act", lambda e: e.copy(out=vp[:, 0:2, 0:256], in_=pA[0][:, :].rearrange("p (h c) -> p h c", h=2)), reads=[pA[0]], writes=[vp])
        wb = load_w(WU, 512, 512)
        mm_proj(pA[1], 512, hC, wb)
        op("act", lambda e: e.copy(out=vp[:, 2:4, 0:256], in_=pA[1][:, :].rearrange("p (h c) -> p h c", h=2)), reads=[pA[1]], writes=[vp])
        wb = load_w(WU, 4096, 8)
        mm_proj(pA[0], 8, hC, wb, bias_off=BI_IF)
        op("act", lambda e: e.copy(out=sml[:, 0:4], in_=pA[0][:, 0:4]), reads=[pA[0]], writes=[sml])
        op("act", lambda e: e.activation(out=sml[:, 4:8], in_=pA[0][:, 4:8], func=AF.Exp, scale=-1.0), reads=[pA[0]], writes=[sml])
        op("act", lambda e: e.activation(out=sml[:, 4:8], in_=sml[:, 4:8], func=AF.Ln, bias=1.0), reads=[sml], writes=[sml])
        op("dve", lambda e: e.tensor_scalar(out=sml[:, 4:8], in0=sml[:, 4:8], scalar1=-1.0, scalar2=None, op0=ALU.mult), reads=[sml], writes=[sml])
        ps = pX[0]
        op("pe", lambda e: e.matmul(ps[:, 0:4], lhsT=CFm('mI'), rhs=sml[:, 4:8], start=True, stop=True), reads=[cf, sml], writes=[ps])
        op("pe", lambda e: e.matmul(ps[:, 4:8], lhsT=CFm('mSL'), rhs=sml[:, 4:8], start=True, stop=True), reads=[cf, sml], writes=[ps])
        op("pe", lambda e: e.matmul(ps[:, 8:12], lhsT=CFm('ones'), rhs=sml[:, 4:8], start=True, stop=True), reads=[cf, sml], writes=[ps])
        op("act", lambda e: e.activation(out=sml[:, 8:12], in_=ps[:, 0:4], func=AF.Exp), reads=[ps], writes=[sml])
        op("dve", lambda e: e.tensor_tensor(out=sml[:, 12:16], in0=ps[:, 4:8], in1=sml[:, 0:4], op=ALU.add), reads=[ps, sml], writes=[sml])
        op("act", lambda e: e.activation(out=sml[:, 12:16], in_=sml[:, 12:16], func=AF.Exp), reads=[sml], writes=[sml])
        op("act", lambda e: e.activation(out=sml[:, 16:20], in_=ps[:, 8:12], func=AF.Exp), reads=[ps], writes=[sml])
        ymt = Fs[3]
        for h in range(4):
            op("dve", lambda e: e.tensor_scalar(out=lhm[:], in0=CFm('mSL'), scalar1=sml[:, 4 + h:5 + h], scalar2=None, op0=ALU.mult),
               reads=[cf, sml], writes=[lhm])
            p0 = pX[0]
            op("pe", lambda e: e.matmul(p0[:, 0:128], lhsT=lhm[:], rhs=CFm('mI'), start=True, stop=True), reads=[lhm, cf], writes=[p0])
            op("act", lambda e: e.activation(out=dmat[:], in_=p0[:, 0:128], func=AF.Exp, bias=sml[:, h:h + 1]), reads=[p0, sml], writes=[dmat])
            op("dve", lambda e: e.tensor_tensor(out=dmat[:], in0=dmat[:], in1=CFm('mI'), op=ALU.mult), reads=[dmat, cf], writes=[dmat])
            p1 = pX[1]
            for dc in range(2):
                op("pe", lambda e, dc=dc: e.matmul(p1[:, 0:128], lhsT=kTt[:, (h * 2 + dc) * 128:(h * 2 + dc + 1) * 128],
                                                  rhs=qTt[:, (h * 2 + dc) * 128:(h * 2 + dc + 1) * 128], start=(dc == 0), stop=(dc == 1)),
                   reads=[kTt, qTt], writes=[p1])
            op("dve", lambda e: e.tensor_tensor(out=pmat[:], in0=p1[:, 0:128], in1=dmat[:], op=ALU.mult), reads=[p1, dmat], writes=[pmat])
            p2 = pX[2]
            op("pe", lambda e: e.matmul(p2[:, 0:257], lhsT=pmat[:], rhs=vp[:, h, 0:257], start=True, stop=True), reads=[pmat, vp], writes=[p2])
            p3 = pX[3]
            for dc in range(2):
                op("pe", lambda e, dc=dc: e.matmul(p3[:, 0:257], lhsT=qTt[:, (h * 2 + dc) * 128:(h * 2 + dc + 1) * 128],
                                                  rhs=st_mlb[:, h, dc, :], start=(dc == 0), stop=(dc == 1)), reads=[qTt, st_mlb], writes=[p3])
            op("act", lambda e: e.copy(out=asb[:], in_=p2[:, 0:257]), reads=[p2], writes=[asb])
            op("dve", lambda e: e.scalar_tensor_tensor(out=num[:], in0=p3[:, 0:257], scalar=sml[:, 8 + h:9 + h], in1=asb[:],
                                                       op0=ALU.mult, op1=ALU.add), reads=[p3, sml, asb], writes=[num])
            op("act", lambda e: e.activation(out=sml[:, 20:21], in_=num[:, 256:257], func=AF.Abs), reads=[num], writes=[sml])
            op("dve", lambda e: e.tensor_scalar(out=sml[:, 20:21], in0=sml[:, 20:21], scalar1=1.0, scalar2=None, op0=ALU.max), reads=[sml], writes=[sml])
            op("dve", lambda e: e.reciprocal(out=sml[:, 21:22], in_=sml[:, 20:21]), reads=[sml], writes=[sml])
            op("dve", lambda e: e.tensor_scalar(out=ymt[:, h * 256:(h + 1) * 256], in0=num[:, 0:256], scalar1=sml[:, 21:22], scalar2=None, op0=ALU.mult),
               reads=[num, sml], writes=[ymt])
            op("dve", lambda e: e.tensor_scalar(out=ktl[:], in0=ktm[:, h * 256:(h + 1) * 256], scalar1=sml[:, 12 + h:13 + h], scalar2=None, op0=ALU.mult),
               reads=[ktm, sml], writes=[ktl])
            for dc in range(2):
                pd = pX[dc]
                op("pe", lambda e, dc=dc: e.matmul(pd[:, 0:257], lhsT=ktl[:, dc * 128:(dc + 1) * 128], rhs=vp[:, h, 0:257], start=True, stop=True),
                   reads=[ktl, vp], writes=[pd])
                op("dve", lambda e, dc=dc: e.scalar_tensor_tensor(out=st_ml[:, h, dc, :], in0=st_ml[:, h, dc, :], scalar=sml[:, 16 + h:17 + h],
                                                                 in1=pd[:, 0:257], op0=ALU.mult, op1=ALU.add), reads=[st_ml, sml, pd], writes=[st_ml])
                op("act", lambda e, dc=dc: e.copy(out=st_mlb[:, h, dc, :], in_=st_ml[:, h, dc, :]), reads=[st_ml], writes=[st_mlb])
        fw.dma("sp", y_m[ti, :, :], ymt[:], reads=[ymt], writes=[y_m], sem_buf=ymt)

        if STAGE < 5:
            continue
        rkv = [Fs[4], Fs[5], Fs[6]]
        for j3 in range(3):
            dstF = rkv[j3]
            for half in range(2):
                c0 = 2048 + j3 * 1024 + half * 512
                wb = load_w(WS, c0, 512)
                mm_proj(pA[half], 512, hC, wb)
                mm_proj(pH, 512, hH, wb)
                uC, uCb, uHb = Fs[0], Hs[2], Hs[3]
                op("act", lambda e: e.copy(out=uC[:, 0:512], in_=pA[half][:, :]), reads=[pA[half]], writes=[uC])
                op("dve", lambda e: e.tensor_copy(out=uCb[:, 0:512], in_=uC[:, 0:512]), reads=[uC], writes=[uCb])
                op("act", lambda e: e.copy(out=uHb[:, 0:512], in_=pH[:, :]), reads=[pH], writes=[uHb])
                for qq in range(2):
                    q = half * 2 + qq
                    op("pe", lambda e: e.matmul(pH[:, qq * 256:(qq + 1) * 256], lhsT=sh[:, shb + q * 2, :], rhs=uCb[:, qq * 256:(qq + 1) * 256],
                                                start=True, stop=False), reads=[sh, uCb], writes=[pH])
                    op("pe", lambda e: e.matmul(pH[:, qq * 256:(qq + 1) * 256], lhsT=sh[:, shb + q * 2 + 1, :], rhs=uHb[:, qq * 256:(qq + 1) * 256],
                                                start=False, stop=True), reads=[sh, uHb], writes=[pH])
                tmp = Fs[1]
                op("dve", lambda e: e.tensor_tensor(out=tmp[:, 0:512], in0=pH[:, :], in1=uC[:, 0:512], op=ALU.subtract), reads=[pH, uC], writes=[tmp])
                mo = MU0 + j3 * 1024 + half * 512
                op("dve", lambda e: e.tensor_tensor(out=tmp[:, 0:512], in0=tmp[:, 0:512], in1=bc[:, mo:mo + 512], op=ALU.mult), reads=[tmp, bc], writes=[tmp])
                op("dve", lambda e: e.tensor_tensor(out=dstF[:, half * 512:half * 512 + 512], in0=tmp[:, 0:512], in1=uC[:, 0:512], op=ALU.add),
                   reads=[tmp, uC], writes=[dstF])
        rT_, kT_, vT_ = rkv
        wb = load_w(WS, 5120, 192)
        mm_proj(pA[0], 192, hC, wb)
        mm_proj(pH, 192, hH, wb)
        uC, uCb, uHb = Fs[0], Hs[2], Hs[3]
        op("act", lambda e: e.copy(out=uC[:, 0:192], in_=pA[0][:, 0:192]), reads=[pA[0]], writes=[uC])
        op("dve", lambda e: e.tensor_copy(out=uCb[:, 0:192], in_=uC[:, 0:192]), reads=[uC], writes=[uCb])
        op("act", lambda e: e.copy(out=uHb[:, 0:192], in_=pH[:, 0:192]), reads=[pH], writes=[uHb])
        for q in range(4):
            op("pe", lambda e: e.matmul(pH[:, 256 + q * 48:256 + (q + 1) * 48], lhsT=sh[:, shb + q * 2, :], rhs=uCb[:, q * 48:(q + 1) * 48],
                                        start=True, stop=False), reads=[sh, uCb], writes=[pH])
            op("pe", lambda e: e.matmul(pH[:, 256 + q * 48:256 + (q + 1) * 48], lhsT=sh[:, shb + q * 2 + 1, :], rhs=uHb[:, q * 48:(q + 1) * 48],
                                        start=False, stop=True), reads=[sh, uHb], writes=[pH])
        xwa = Fs[1]
        op("dve", lambda e: e.tensor_tensor(out=xwa[:, 0:192], in0=pH[:, 256:448], in1=uC[:, 0:192], op=ALU.subtract), reads=[pH, uC], writes=[xwa])
        op("dve", lambda e: e.tensor_tensor(out=xwa[:, 0:192], in0=xwa[:, 0:192], in1=bc[:, MU0 + 3072:MU0 + 3264], op=ALU.mult), reads=[xwa, bc], writes=[xwa])
        op("dve", lambda e: e.tensor_tensor(out=xwa[:, 0:192], in0=xwa[:, 0:192], in1=uC[:, 0:192], op=ALU.add), reads=[xwa, uC], writes=[xwa])
        lx = Hs[2]
        xv = xwa[:, 0:192].rearrange("p (q c) -> p q c", q=4)
        lv = lx[:, 0:192].rearrange("p (q c) -> p q c", q=4)
        op("act", lambda e: e.activation(out=lv[:, :, 0:16], in_=xv[:, :, 0:16], func=AF.Tanh), reads=[xwa], writes=[lx])
        op("act", lambda e: e.copy(out=lv[:, :, 16:48], in_=xv[:, :, 16:48]), reads=[xwa], writes=[lx])
        lxT = Hs[3]
        op("pe", lambda e: e.transpose(out=pT[:, 0:128], in_=lx[:, 0:128], identity=identb[:]), reads=[lx, identb], writes=[pT])
        op("pe", lambda e: e.transpose(out=pT[0:64, 128:256], in_=lx[:, 128:192], identity=identb[:]), reads=[lx, identb], writes=[pT])
        op("act", lambda e: e.copy(out=lxT[:, 0:128], in_=pT[:, 0:128]), reads=[pT], writes=[lxT])
        op("act", lambda e: e.copy(out=lxT[0:64, 128:256], in_=pT[0:64, 128:256]), reads=[pT], writes=[lxT])
        sigw, a0t, a1t = Fs[7], Fs[8], Fs[0]
        for (j, dstF, boff) in ((0, sigw, BI_W0), (1, a0t, BI_A0), (2, a1t, BI_A0 + 1024)):
            for half in range(2):
                ps_ = pA[half]
                op("pe", lambda e: e.matmul(ps_[:, :], lhsT=lxT[:, 0:128], rhs=w2s[:, 0, j, half * 512:half * 512 + 512], start=True, stop=False),
                   reads=[lxT, w2s], writes=[ps_])
                op("pe", lambda e: e.matmul(ps_[:, :], lhsT=lxT[0:64, 128:256], rhs=w2s[0:64, 1, j, half * 512:half * 512 + 512], start=False, stop=False),
                   reads=[lxT, w2s], writes=[ps_])
                op("pe", lambda e: e.matmul(ps_[:, :], lhsT=ones33[:, :], rhs=bias[:, boff + half * 512:boff + half * 512 + 512], start=False, stop=True),
                   reads=[ones33, bias], writes=[ps_])
                op("act", lambda e: e.activation(out=dstF[:, half * 512:half * 512 + 512], in_=ps_[:, :], func=AF.Sigmoid), reads=[ps_], writes=[dstF])
        if L1:
            wb = load_w(WU, 4104, 32)
            mm_proj(pA[0], 32, hC, wb)
            hv1 = Hs[2]
            op("act", lambda e: e.copy(out=hv1[:, 0:32], in_=pA[0][:, 0:32]), reads=[pA[0]], writes=[hv1])
            op("pe", lambda e: e.transpose(out=pT[0:32, 0:128], in_=hv1[:, 0:32], identity=identb[:]), reads=[hv1, identb], writes=[pT])
            hv1T = Hs[3]
            op("act", lambda e: e.copy(out=hv1T[0:32, 0:128], in_=pT[0:32, 0:128]), reads=[pT], writes=[hv1T])
            vg = Fs[1]
            for half in range(2):
                ps_ = pA[half]
                op("pe", lambda e: e.matmul(ps_[:, :], lhsT=hv1T[0:32, 0:128], rhs=v2s[:, half * 512:half * 512 + 512], start=True, stop=False),
                   reads=[hv1T, v2s], writes=[ps_])
                op("pe", lambda e: e.matmul(ps_[:, :], lhsT=ones33[:, :], rhs=bias[:, BI_V0 + half * 512:BI_V0 + half * 512 + 512], start=False, stop=True),
                   reads=[ones33, bias], writes=[ps_])
                op("act", lambda e: e.activation(out=vg[:, half * 512:half * 512 + 512], in_=ps_[:, :], func=AF.Sigmoid), reads=[ps_], writes=[vg])
            vfl = Fs[2]
            fw.dma("sp", vfl[:], vfin[ti, :, :], reads=[vfin], writes=[vfl])
            op("dve", lambda e: e.tensor_tensor(out=vfl[:], in0=vfl[:], in1=vT_[:], op=ALU.subtract), reads=[vfl, vT_], writes=[vfl])
            op("dve", lambda e: e.tensor_tensor(out=vfl[:], in0=vfl[:], in1=vg[:], op=ALU.mult), reads=[vfl, vg], writes=[vfl])
            op("dve", lambda e: e.tensor_tensor(out=vT_[:], in0=vT_[:], in1=vfl[:], op=ALU.add), reads=[vT_, vfl], writes=[vT_])
        else:
            fw.dma("sp", vf_o[ti, :, :], vT_[:], reads=[vT_], writes=[vf_o], sem_buf=vT_)
        kkr, sq = Fs[1], Fs[2]
        op("dve", lambda e: e.tensor_tensor(out=kkr[:], in0=kT_[:], in1=bc[:, KK0:KK0 + 1024], op=ALU.mult), reads=[kT_, bc], writes=[kkr])
        op("act", lambda e: e.activation(out=sq[:], in_=kkr[:], func=AF.Square), reads=[kkr], writes=[sq])
        op("dve", lambda e: e.tensor_reduce(out=small[:, 0:16], in_=sq[:].rearrange("p (h c) -> p h c", h=16), axis=AX.X, op=ALU.add),
           reads=[sq], writes=[small])
        op("act", lambda e: e.activation(out=small[:, 0:16], in_=small[:, 0:16], func=AF.Sqrt), reads=[small], writes=[small])
        op("dve", lambda e: e.tensor_scalar(out=small[:, 0:16], in0=small[:, 0:16], scalar1=1e-12, scalar2=None, op0=ALU.max), reads=[small], writes=[small])
        op("dve", lambda e: e.reciprocal(out=small[:, 16:32], in_=small[:, 0:16]), reads=[small], writes=[small])
        op("dve", lambda e: e.tensor_tensor(out=kkr[:].rearrange("p (h c) -> p h c", h=16), in0=kkr[:].rearrange("p (h c) -> p h c", h=16),
                                            in1=small[:, 16:32].unsqueeze(2).broadcast_to([128, 16, 64]), op=ALU.mult), reads=[kkr, small], writes=[kkr])
        kk_ = kkr
        t1 = Fs[2]
        op("dve", lambda e: e.tensor_tensor(out=t1[:], in0=a0t[:], in1=a1t[:], op=ALU.add), reads=[a0t, a1t], writes=[t1])
        op("dve", lambda e: e.scalar_tensor_tensor(out=t1[:], in0=t1[:], scalar=-2.0, in1=bc[:, KK0 + 1024:KK0 + 2048], op0=ALU.add, op1=ALU.mult),
           reads=[t1, bc], writes=[t1])
        op("dve", lambda e: e.scalar_tensor_tensor(out=t1[:], in0=t1[:], scalar=2.0, in1=kT_[:], op0=ALU.add, op1=ALU.mult), reads=[t1, kT_], writes=[t1])
        op("dve", lambda e: e.tensor_tensor(out=t1[:], in0=t1[:], in1=rT_[:], op=ALU.mult), reads=[t1, rT_], writes=[t1])
        fw.dma("sp", cvb[:, 0:1024], bcin[0, 5312:6336].partition_broadcast(128), reads=[bcin], writes=[cvb])
        op("dve", lambda e: e.tensor_tensor(out=t1[:], in0=t1[:], in1=cvb[:, 0:1024], op=ALU.mult), reads=[t1, cvb], writes=[t1])
        op("dve", lambda e: e.tensor_reduce(out=small[:, 32:48], in_=t1[:].rearrange("p (h c) -> p h c", h=16), axis=AX.X, op=ALU.add),
           reads=[t1], writes=[small])
        op("dve", lambda e: e.tensor_tensor(out=t1[:].rearrange("p (h c) -> p h c", h=16), in0=vT_[:].rearrange("p (h c) -> p h c", h=16),
                                            in1=small[:, 32:48].unsqueeze(2).broadcast_to([128, 16, 64]), op=ALU.mult), reads=[vT_, small], writes=[t1])
        fw.dma("sp", bon_o[ti, :, :], t1[:], reads=[t1], writes=[bon_o], sem_buf=t1)
        kd_ = Fs[9]
        op("dve", lambda e: e.scalar_tensor_tensor(out=kd_[:], in0=a0t[:], scalar=-1.0, in1=bc[:, KK0 + 1024:KK0 + 2048], op0=ALU.add, op1=ALU.mult),
           reads=[a0t, bc], writes=[kd_])
        op("dve", lambda e: e.scalar_tensor_tensor(out=kd_[:], in0=kd_[:], scalar=1.0, in1=kT_[:], op0=ALU.add, op1=ALU.mult), reads=[kd_, kT_], writes=[kd_])
        EG, ENG, EGX, E2B = Hs[6], Hs[7], Hs[8], Hs[9]
        EGf, ENGf, EGXf, E2Bf = Fs[0], Fs[2], Fs[3], Fs[1]
        E2Bf = kT_
        for half in range(2):
            for (nm, dstF, sc) in (('rk_inc', EGf, 1.0), ('rk_inc', ENGf, -1.0), ('rk_exc', EGXf, 1.0), ('rk_sel2', E2Bf, 1.0)):
                ps_ = pA[half]
                op("pe", lambda e: e.matmul(ps_[:, :], lhsT=CFm(nm), rhs=sigw[:, half * 512:half * 512 + 512], start=True, stop=True),
                   reads=[cf, sigw], writes=[ps_])
                op("act", lambda e: e.activation(out=dstF[:, half * 512:half * 512 + 512], in_=ps_[:, :], func=AF.Exp, scale=sc), reads=[ps_], writes=[dstF])
        pe_ = pX[0]
        for h in range(16):
            op("pe", lambda e, h=h: e.matmul(pe_[0:64, h * 2:h * 2 + 2], lhsT=sigw[:, h * 64:(h + 1) * 64], rhs=cf[:, cfi['rk_selv'], 0:2], start=True, stop=True),
               reads=[sigw, cf], writes=[pe_])
        op("act", lambda e: e.activation(out=e12[:, :, 0:2], in_=pe_[0:64, 0:32].rearrange("p (h c) -> p h c", h=16), func=AF.Exp), reads=[pe_], writes=[e12])
        op("dve", lambda e: e.tensor_tensor(out=e12[:, :, 2:3], in0=e12[:, :, 0:1], in1=e12[:, :, 1:2], op=ALU.mult), reads=[e12], writes=[e12])
        rtl, ktlr, btl, atl, bte2, vb = Hs[6], Hs[7], Hs[8], Hs[9], Hs[10], Hs[11]
        op("dve", lambda e: e.tensor_tensor(out=rtl[:], in0=rT_[:], in1=EGf[:], op=ALU.mult), reads=[rT_, EGf], writes=[rtl])
        op("dve", lambda e: e.tensor_tensor(out=ktlr[:], in0=kd_[:], in1=ENGf[:], op=ALU.mult), reads=[kd_, ENGf], writes=[ktlr])
        op("dve", lambda e: e.tensor_tensor(out=ENGf[:], in0=ENGf[:], in1=kk_[:], op=ALU.mult), reads=[ENGf, kk_], writes=[ENGf])
        op("dve", lambda e: e.tensor_tensor(out=ENGf[:], in0=ENGf[:], in1=a0t[:], op=ALU.mult), reads=[ENGf, a0t], writes=[ENGf])
        op("act", lambda e: e.copy(out=btl[:], in_=ENGf[:]), reads=[ENGf], writes=[btl])
        op("dve", lambda e: e.tensor_tensor(out=bte2[:], in0=ENGf[:], in1=E2Bf[:], op=ALU.mult), reads=[ENGf, E2Bf], writes=[bte2])
        op("dve", lambda e: e.scalar_tensor_tensor(out=atl[:], in0=kk_[:], scalar=-1.0, in1=EGXf[:], op0=ALU.mult, op1=ALU.mult), reads=[kk_, EGXf], writes=[atl])
        op("act", lambda e: e.copy(out=vb[:], in_=vT_[:]), reads=[vT_], writes=[vb])
        for (srcb, put) in ((atl, lambda g: art[:, g, 0:128]), (rtl, lambda g: art[:, g, 128:256]), (btl, lambda g: btT[:, g, :]), (ktlr, lambda g: ktT[:, g, :])):
            for g in range(8):
                op("pe", lambda e, g=g: e.transpose(out=pT[:, g * 128:(g + 1) * 128], in_=srcb[:, g * 128:(g + 1) * 128], identity=identb[:]),
                   reads=[srcb, identb], writes=[pT])
            dstb = art if srcb in (atl, rtl) else (btT if srcb is btl else ktT)
            if srcb is atl:
                op("act", lambda e: e.copy(out=art[:, :, 0:128], in_=pT[:, :].rearrange("p (g t) -> p g t", g=8)), reads=[pT], writes=[art])
            elif srcb is rtl:
                op("act", lambda e: e.copy(out=art[:, :, 128:256], in_=pT[:, :].rearrange("p (g t) -> p g t", g=8)), reads=[pT], writes=[art])
            else:
                op("act", lambda e: e.copy(out=dstb[:, :, :], in_=pT[:, :].rearrange("p (g t) -> p g t", g=8)), reads=[pT], writes=[dstb])
        if STAGE < 6:
            continue
        yrt = Fs[0]
        for h in range(16):
            g, po = h // 2, (h % 2) * 64
            hs = slice(h * 64, (h + 1) * 64)
            p0, p1, p2, p3 = pX
            op("pe", lambda e: e.matmul(p0[:, 0:256], lhsT=btT[po:po + 64, g, :], rhs=art[po:po + 64, g, :], start=True, stop=True), reads=[btT, art], writes=[p0])
            op("pe", lambda e: e.matmul(p0[:, 256:512], lhsT=ktT[po:po + 64, g, :], rhs=art[po:po + 64, g, :], start=True, stop=True), reads=[ktT, art], writes=[p0])
            op("pe", lambda e: e.matmul(p1[:, 0:128], lhsT=art[po:po + 64, g, 0:128], rhs=btT[po:po + 64, g, :], start=True, stop=True), reads=[art, btT], writes=[p1])
            op("dve", lambda e: e.tensor_tensor(out=nbab[:], in0=p0[:, :], in1=mask4[:], op=ALU.mult), reads=[p0, mask4], writes=[nbab])
            op("dve", lambda e: e.tensor_tensor(out=xq[0][:], in0=p1[:, 0:128], in1=CFm('mSL'), op=ALU.mult), reads=[p1, cf], writes=[xq[0]])
            op("pe", lambda e: e.matmul(p2[:, 0:64], lhsT=nbab[:, 256:384], rhs=vb[:, hs], start=True, stop=True), reads=[nbab, vb], writes=[p2])
            op("act", lambda e: e.copy(out=rf[:, 0:64], in_=atl[:, hs]), reads=[atl], writes=[rf])
            op("act", lambda e: e.copy(out=rf[:, 64:128], in_=p2[:, 0:64]), reads=[p2], writes=[rf])
            op("dve", lambda e: e.tensor_copy(out=rb[:], in_=rf[:]), reads=[rf], writes=[rb])
            Pc, Xc = nbab[:, 0:128], xq[0]
            Pbuf = nbab
            for lvl in range(7):
                if lvl > 0:
                    Pn, Xn = pq[lvl % 2], xq[lvl % 2]
                    op("pe", lambda e: e.matmul(p2[:, 0:128], lhsT=Xc[:], rhs=Pc, start=True, stop=True), reads=[Xc, Pbuf], writes=[p2])
                    if lvl < 6:
                        op("pe", lambda e: e.matmul(p2[:, 128:256], lhsT=Pc, rhs=Xc[:], start=True, stop=True), reads=[Xc, Pbuf], writes=[p2])
                    op("act", lambda e: e.copy(out=Pn[:], in_=p2[:, 0:128]), reads=[p2], writes=[Pn])
                    if lvl < 6:
                        op("dve", lambda e: e.tensor_copy(out=Xn[:], in_=p2[:, 128:256]), reads=[p2], writes=[Xn])
                    Pc, Xc, Pbuf = Pn[:], Xn, Pn
                op("pe", lambda e: e.matmul(p3[:, 0:128], lhsT=Pc, rhs=rb[:], start=True, stop=True), reads=[Pbuf, rb], writes=[p3])
                op("dve", lambda e: e.tensor_tensor(out=rf[:], in0=rf[:], in1=p3[:, 0:128], op=ALU.add), reads=[rf, p3], writes=[rf])
                op("act", lambda e: e.copy(out=rb[:], in_=rf[:]), reads=[rf], writes=[rb])
            op("pe", lambda e: e.matmul(p0[:, 0:64], lhsT=nbab[:, 128:256], rhs=rb[:, 0:64], start=True, stop=True), reads=[nbab, rb], writes=[p0])
            op("pe", lambda e: e.matmul(p0[:, 64:128], lhsT=nbab[:, 128:256], rhs=rb[:, 64:128], start=True, stop=False), reads=[nbab, rb], writes=[p0])
            op("pe", lambda e: e.matmul(p0[:, 64:128], lhsT=nbab[:, 384:512], rhs=vb[:, hs], start=False, stop=True), reads=[nbab, vb], writes=[p0])
            op("dve", lambda e: e.tensor_tensor(out=m2b[:], in0=p0[:, 0:64], in1=rtl[:, hs], op=ALU.add), reads=[p0, rtl], writes=[m2b])
            op("act", lambda e: e.copy(out=ycs[:], in_=p0[:, 64:128]), reads=[p0], writes=[ycs])
            op("pe", lambda e: e.transpose(out=pT[0:64, 0:128], in_=m2b[:], identity=identb[:]), reads=[m2b, identb], writes=[pT])
            op("act", lambda e: e.activation(out=m2T[:], in_=pT[0:64, 0:128], func=AF.Identity, scale=e12[:, h, 0:1]), reads=[pT, e12], writes=[m2T])
            op("pe", lambda e: e.matmul(p1[0:64, 0:64], lhsT=rb[:, 0:64], rhs=bte2[:, hs], start=True, stop=True), reads=[rb, bte2], writes=[p1])
            op("dve", lambda e: e.tensor_scalar(out=i64e[:], in0=cf[0:64, cfi['ident'], 0:64], scalar1=e12[:, h, 2:3], scalar2=None, op0=ALU.mult),
               reads=[cf, e12], writes=[i64e])
            op("dve", lambda e: e.scalar_tensor_tensor(out=Lb[:], in0=p1[0:64, 0:64], scalar=e12[:, h, 0:1], in1=i64e[:], op0=ALU.mult, op1=ALU.add),
               reads=[p1, e12, i64e], writes=[Lb])
            op("pe", lambda e: e.matmul(p1[0:64, 64:128], lhsT=btl[:, hs], rhs=rb[:, 64:128], start=True, stop=False), reads=[btl, rb], writes=[p1])
            op("pe", lambda e: e.matmul(p1[0:64, 64:128], lhsT=ktlr[:, hs], rhs=vb[:, hs], start=False, stop=True), reads=[ktlr, vb], writes=[p1])
            op("act", lambda e: e.activation(out=dcp[:], in_=p1[0:64, 64:128], func=AF.Identity, scale=e12[:, h, 1:2]), reads=[p1, e12], writes=[dcp])
            op("pe", lambda e: e.matmul(p2[:, 256:320], lhsT=m2T[:], rhs=st_rkb[:, h, :], start=True, stop=True), reads=[m2T, st_rkb], writes=[p2])
            op("dve", lambda e: e.tensor_tensor(out=yrt[:, hs], in0=p2[:, 256:320], in1=ycs[:], op=ALU.add), reads=[p2, ycs], writes=[yrt])
            op("pe", lambda e: e.matmul(p3[0:64, 256:320], lhsT=Lb[:], rhs=st_rkb[:, h, :], start=True, stop=True), reads=[Lb, st_rkb], writes=[p3])
            op("dve", lambda e: e.tensor_tensor(out=st_rk[:, h, :], in0=p3[0:64, 256:320], in1=dcp[:], op=ALU.add), reads=[p3, dcp], writes=[st_rk])
            op("act", lambda e: e.copy(out=st_rkb[:, h, :], in_=st_rk[:, h, :]), reads=[st_rk], writes=[st_rkb])
        fw.dma("sp", y_r[ti, :, :], yrt[:], reads=[yrt], writes=[y_r], sem_buf=yrt)

        if STAGE < 7:
            continue
        hq, hv_, lf, kin = Hs[2], Hs[3], Fs[1], Fs[2]
        for half in range(2):
            wb = load_w(WU, 1024 + half * 512, 512)
            mm_proj(pA[half], 512, hC, wb)
            op("act", lambda e: e.activation(out=hq[:, half * 512:half * 512 + 512], in_=pA[half][:, :], func=AF.Silu), reads=[pA[half]], writes=[hq])
        for half in range(2):
            wb = load_w(WU, 3072 + half * 512, 512)
            mm_proj(pA[half], 512, hC, wb)
            op("act", lambda e: e.copy(out=hv_[:, half * 512:half * 512 + 512], in_=pA[half][:, :]), reads=[pA[half]], writes=[hv_])
        for half in range(2):
            wb = load_w(WU, 2048 + half * 512, 512)
            mm_proj(pA[half], 512, hC, wb, bias_off=BI_F + half * 512)
            sl = slice(half * 512, half * 512 + 512)
            sg = Fs[3]
            op("act", lambda e: e.activation(out=sg[:, sl], in_=pA[half][:, :], func=AF.Sigmoid), reads=[pA[half]], writes=[sg])
            op("dve", lambda e: e.tensor_scalar(out=kin[:, sl], in0=sg[:, sl], scalar1=-1.0, scalar2=1.0, op0=ALU.mult, op1=ALU.add), reads=[sg], writes=[kin])
            op("dve", lambda e: e.tensor_tensor(out=lf[:, sl], in0=kin[:, sl], in1=bc[:, LB0 + half * 512:LB0 + half * 512 + 512], op=ALU.mult),
               reads=[kin, bc], writes=[lf])
            op("dve", lambda e: e.tensor_tensor(out=kin[:, sl], in0=kin[:, sl], in1=lf[:, sl], op=ALU.subtract), reads=[kin, lf], writes=[kin])
            op("act", lambda e: e.activation(out=lf[:, sl], in_=kin[:, sl], func=AF.Ln, scale=-1.0, bias=1.0), reads=[kin], writes=[lf])
        khg = Hs[6]
        for half in range(2):
            ps_ = pA[half]
            op("pe", lambda e: e.matmul(ps_[:, :], lhsT=CFm('hg_inc'), rhs=lf[:, half * 512:half * 512 + 512], start=True, stop=True), reads=[cf, lf], writes=[ps_])
            tmpe = Fs[3]
            op("act", lambda e: e.activation(out=tmpe[:, half * 512:half * 512 + 512], in_=ps_[:, :], func=AF.Exp, scale=-1.0), reads=[ps_], writes=[tmpe])
            op("dve", lambda e: e.tensor_tensor(out=khg[:, half * 512:half * 512 + 512], in0=tmpe[:, half * 512:half * 512 + 512],
                                                in1=kin[:, half * 512:half * 512 + 512], op=ALU.mult), reads=[tmpe, kin], writes=[khg])
        kinb = Hs[7]
        op("act", lambda e: e.copy(out=kinb[:], in_=kin[:]), reads=[kin], writes=[kinb])
        yht = Fs[0]
        for h in range(8):
            hs = slice(h * 128, (h + 1) * 128)
            p0, p1, p2, p3 = pX
            op("pe", lambda e: e.matmul(p0[:, 0:128], lhsT=lf[:, hs], rhs=CFm('hg_inc'), start=True, stop=True), reads=[lf, cf], writes=[p0])
            op("pe", lambda e: e.matmul(p0[:, 128:130], lhsT=lf[:, hs], rhs=cf[:, cfi['hg_selv'], 0:2], start=True, stop=True), reads=[lf, cf], writes=[p0])
            op("act", lambda e: e.activation(out=hgeg[:], in_=p0[:, 0:128], func=AF.Exp), reads=[p0], writes=[hgeg])
            op("act", lambda e: e.activation(out=hgen[:], in_=p0[:, 0:128], func=AF.Exp, scale=-1.0), reads=[p0], writes=[hgen])
            op("act", lambda e: e.activation(out=hge[:, h, :], in_=p0[:, 128:130], func=AF.Exp), reads=[p0], writes=[hge])
            op("pe", lambda e: e.transpose(out=pT[:, 0:128], in_=hq[:, hs], identity=identb[:]), reads=[hq, identb], writes=[pT])
            op("pe", lambda e: e.transpose(out=pT[:, 128:256], in_=kinb[:, hs], identity=identb[:]), reads=[kinb, identb], writes=[pT])
            op("dve", lambda e: e.tensor_tensor(out=qtl[:], in0=pT[:, 0:128], in1=hgeg[:], op=ALU.mult), reads=[pT, hgeg], writes=[qtl])
            op("dve", lambda e: e.tensor_tensor(out=kTl[:], in0=pT[:, 128:256], in1=hgen[:], op=ALU.mult), reads=[pT, hgen], writes=[kTl])
            op("pe", lambda e: e.matmul(p1[:, 0:128], lhsT=kTl[:], rhs=qtl[:], start=True, stop=True), reads=[kTl, qtl], writes=[p1])
            op("dve", lambda e: e.tensor_tensor(out=attb[:], in0=p1[:, 0:128], in1=CFm('mI'), op=ALU.mult), reads=[p1, cf], writes=[attb])
            op("dve", lambda e: e.tensor_scalar(out=s0f[:], in0=st_hg[:, h, :], scalar1=hge[:, h, 0:1], scalar2=None, op0=ALU.mult), reads=[st_hg, hge], writes=[s0f])
            op("act", lambda e: e.copy(out=s0t[:], in_=s0f[:]), reads=[s0f], writes=[s0t])
            op("pe", lambda e: e.matmul(p2[:, 0:128], lhsT=attb[:], rhs=hv_[:, hs], start=True, stop=False), reads=[attb, hv_], writes=[p2])
            op("pe", lambda e: e.matmul(p2[:, 0:128], lhsT=qtl[:], rhs=s0t[:], start=False, stop=True), reads=[qtl, s0t], writes=[p2])
            op("act", lambda e: e.copy(out=yht[:, hs], in_=p2[:, 0:128]), reads=[p2], writes=[yht])
            op("pe", lambda e: e.matmul(p3[:, 0:128], lhsT=khg[:, hs], rhs=hv_[:, hs], start=True, stop=True), reads=[khg, hv_], writes=[p3])
            op("dve", lambda e: e.tensor_tensor(out=s0f[:], in0=s0f[:], in1=p3[:, 0:128], op=ALU.add), reads=[s0f, p3], writes=[s0f])
            op("dve", lambda e: e.tensor_scalar(out=st_hg[:, h, :], in0=s0f[:], scalar1=hge[:, h, 1:2], scalar2=None, op0=ALU.mult), reads=[s0f, hge], writes=[st_hg])
        fw.dma("sp", y_h[ti, :, :], yht[:], reads=[yht], writes=[y_h], sem_buf=yht)

    fw.dma("sp", st_ml_out[:], st_ml[:], reads=[st_ml], writes=[st_ml_out], sem_buf=st_ml)
    fw.dma("sp", st_rk_out[:], st_rk[:], reads=[st_rk], writes=[st_rk_out], sem_buf=st_rk)
    fw.dma("sp", st_hg_out[:], st_hg[:], reads=[st_hg], writes=[st_hg_out], sem_buf=st_hg)
    outs = [y_m, y_r, y_h, bon_o, st_ml_out, st_rk_out, st_hg_out] + ([] if L1 else [vf_o])
    fw.wait_all("sp", outs)
    fw.close()
    return nc, fw


def emit_modulation(fw, nc, cfm, adaw, adab, normw, modA, modB, modG, wbuf, ps, Ftmp, Htmp, stgs):
    op = fw.op
    scf, scb = Ftmp, Htmp
    fw.dma("sp", scf[:, 0:16], cfm[:].rearrange("p k s -> p (k s)"), reads=[cfm], writes=[scf])
    op("act", lambda e: e.activation(out=scb[:, 0:16], in_=scf[:, 0:16], func=AF.Silu), reads=[scf], writes=[scb])
    fw.dma("sp", scf[:, 16:40], adab[:], reads=[adab], writes=[scf])
    fw.dma("sp", scf[:, 40:48], normw[:], reads=[normw], writes=[scf])
    for jg in range(6):
        wb = wbuf[jg % 2]
        for k0 in range(0, 8, 2):
            stg = stgs[(jg * 4 + k0 // 2) % len(stgs)]
            sv = stg[:, 0:1024].rearrange("p (k n) -> p k n", k=2)
            fw.dma("sp", sv, adaw[k0 * 128:(k0 + 2) * 128, jg * 512:(jg + 1) * 512].rearrange("(k p) n -> p k n", p=128), reads=[adaw], writes=[stg])
            op("pool", lambda e: e.tensor_copy(out=wb[:, k0:k0 + 2, :], in_=sv), reads=[stg], writes=[wb])
        for jj in range(4):
            jc = jg * 4 + jj
            for k in range(8):
                op("pe", lambda e, k=k: e.matmul(ps[:, jc * 2:jc * 2 + 2], lhsT=wb[:, k, jj * 128:(jj + 1) * 128], rhs=scb[:, k * 2:k * 2 + 2],
                                                 start=(k == 0), stop=(k == 7)), reads=[wb, scb], writes=[ps])
    md = scf[:, 64:112].rearrange("p (j s) -> p j s", s=2)
    op("dve", lambda e: e.tensor_tensor(out=md, in0=ps[:, 0:48].rearrange("p (j s) -> p j s", s=2),
                                        in1=scf[:, 16:40].unsqueeze(2).broadcast_to([128, 24, 2]), op=ALU.add), reads=[ps, scf], writes=[scf])
    op("dve", lambda e: e.tensor_copy(out=modB[:], in_=md[:, 0:8, :]), reads=[scf], writes=[modB])
    op("dve", lambda e: e.scalar_tensor_tensor(out=modA[:], in0=md[:, 8:16, :], scalar=1.0, in1=scf[:, 40:48].unsqueeze(2).broadcast_to([128, 8, 2]),
                                               op0=ALU.add, op1=ALU.mult), reads=[scf], writes=[modA])
    if modG is not None:
        op("dve", lambda e: e.tensor_copy(out=modG[:], in_=md[:, 16:24, :]), reads=[scf], writes=[modG])


def emit_norm_hT(fw, x_, hTt, modA, modB, sidx, small, xn, ident_ap, identbuf, psa, psb):
    op = fw.op
    op("act", lambda e: e.activation(out=xn[:], in_=x_[:], func=AF.Square, accum_out=small[:, 40:41]), reads=[x_], writes=[xn, small])
    op("dve", lambda e: e.tensor_scalar(out=small[:, 41:42], in0=small[:, 40:41], scalar1=1.0 / D, scalar2=EPS, op0=ALU.mult, op1=ALU.add),
       reads=[small], writes=[small])
    op("act", lambda e: e.activation(out=small[:, 41:42], in_=small[:, 41:42], func=AF.Sqrt), reads=[small], writes=[small])
    op("dve", lambda e: e.reciprocal(out=small[:, 42:43], in_=small[:, 41:42]), reads=[small], writes=[small])
    op("dve", lambda e: e.tensor_scalar(out=xn[:], in0=x_[:], scalar1=small[:, 42:43], scalar2=None, op0=ALU.mult), reads=[x_, small], writes=[xn])
    for k in range(8):
        ps = psa if k < 4 else psb
        op("pe", lambda e, k=k: e.transpose(out=ps[:, (k % 4) * 128:(k % 4 + 1) * 128], in_=xn[:, k * 128:(k + 1) * 128], identity=ident_ap),
           reads=[xn, identbuf], writes=[ps])
    for k in range(8):
        ps = psa if k < 4 else psb
        op("act", lambda e, k=k: e.activation(out=hTt[:, k, :], in_=ps[:, (k % 4) * 128:(k % 4 + 1) * 128], func=AF.Identity,
                                              scale=modA[:, k, sidx:sidx + 1], bias=modB[:, k, sidx:sidx + 1]), reads=[ps, modA, modB], writes=[hTt])


def _tile_variants(chain, CT, XT):
    keys = []
    out = []
    for kind, i in chain:
        n = CT if kind == 'c' else XT
        key = (kind, i == 0, i == n - 1)
        if key not in keys:
            keys.append(key)
        out.append((kind, keys.index(key)))
    return out, keys


def scan_core_inputs(P, l, b, d, cs_b, xs_b, chain, CT, XT, states, vf_c_b, vf_x_b):
    f32 = np.float32
    cp = cs_b if d == 0 else cs_b[::-1]
    xp = xs_b if d == 0 else xs_b[::-1]
    tiles, keys = _tile_variants(chain, CT, XT)
    NT = len(chain)
    xseq = np.zeros(((NT + 1) * 128, D), f32)
    for n, (kind, i) in enumerate(chain):
        src, ntl = (xp, XT) if kind == 'x' else (cp, CT)
        xseq[64 + n * 128:64 + (n + 1) * 128] = src[i * 128:(i + 1) * 128]
    k0, i0 = chain[0]
    if i0 > 0:
        xseq[0:64] = (xp if k0 == 'x' else cp)[i0 * 128 - 64:i0 * 128]
    k1, i1 = chain[-1]
    if i1 < (XT if k1 == 'x' else CT) - 1:
        xseq[64 + NT * 128:] = (xp if k1 == 'x' else cp)[(i1 + 1) * 128:(i1 + 1) * 128 + 64]
    w_in = P['w_in'][l]
    o = OFF
    perm = _wdad_perm()
    wdad = np.concatenate([w_in[:, o['r_wd'] + d * 64:o['r_wd'] + d * 64 + 64],
                           w_in[:, o['r_ad'] + d * 64:o['r_ad'] + d * 64 + 64],
                           w_in[:, o['r_ad'] + (1 - d) * 64:o['r_ad'] + (1 - d) * 64 + 64]], axis=1)[:, perm]
    WS = np.concatenate([w_in[:, o['m_q']:o['m_q'] + 2048], w_in[:, o['r_r']:o['r_r'] + 3072], wdad], axis=1)
    wu = [w_in[:, o['m_v']:o['m_v'] + 1024], w_in[:, o['h_q']:o['h_q'] + 1024],
          w_in[:, o['h_f'] + d * 1024:o['h_f'] + (d + 1) * 1024], w_in[:, o['h_i']:o['h_i'] + 1024],
          w_in[:, o['m_if'] + d * 8:o['m_if'] + d * 8 + 8]]
    if l > 0:
        wu.append(P['rk_v1'][l - 1])
    WU = np.concatenate(wu, axis=1)
    cfm = np.stack([P['c'][b].reshape(8, 128).T, P['c_ctx'].reshape(8, 128).T], axis=-1)
    mu = P['rk_mu'][l]
    mu192 = np.concatenate([mu[3072 + d * 64:3072 + d * 64 + 64], mu[3200 + d * 64:3200 + d * 64 + 64],
                            mu[3200 + (1 - d) * 64:3200 + (1 - d) * 64 + 64]])[perm]
    bcin = np.concatenate([mu[0:3072], mu192, P['rk_kk'][l], P['rk_ka'][l], P['rk_rk'][l], P['hg_lb'][d, 0], P['hg_lb'][d, 1]])[None, :]
    cw = P['ml_conv'][l]
    cv = []
    for g in range(4):
        w = cw[g // 2][:, (g % 2) * 512:(g % 2) * 512 + 512]
        taps = (w[0], w[1], w[2]) if d == 0 else (w[2], w[1], w[0])
        cv += list(taps)
    convin = np.concatenate(cv)[None, :]
    bl = [P['ml_if_b'][l, d].reshape(8), P['hg_f_b'][l, d], P['rk_w0'][l, d], P['rk_a0'][l, d], P['rk_a0'][l, 1 - d]]
    if l > 0:
        bl.append(P['rk_v0'][l - 1])
    biasin = np.concatenate(bl)[None, :]
    W2 = np.zeros((192, 3, 1024), f32)
    for p_ in range(192):
        s_ = perm[p_]
        sub, r = s_ // 64, s_ % 64
        if sub == 0:
            W2[p_, 0] = P['rk_w2'][l, d][r]
        elif sub == 1:
            W2[p_, 1] = P['rk_a2'][l, d][r]
        else:
            W2[p_, 2] = P['rk_a2'][l, 1 - d][r]
    cfc = _const_f32()
    CF = np.stack([cfc[n] for n in CF_NAMES])
    SH = np.concatenate([_shift_mats(k[0], k[1], k[2], d) for k in keys])
    m = dict(xseq=xseq, WS=np.ascontiguousarray(WS), WU=np.ascontiguousarray(WU), cfm=np.ascontiguousarray(cfm.astype(f32)),
             adaw=P['ada_w'][l], adab=np.ascontiguousarray(P['ada_b'][l].reshape(24, 128).T),
             normw=np.ascontiguousarray(P['norm_w'][l].reshape(8, 128).T), CF=CF, SH=SH.astype(f32),
             bcin=bcin.astype(f32), convin=convin.astype(f32), biasin=biasin.astype(f32), W2=W2,
             st_ml_in=states[0], st_rk_in=states[1], st_hg_in=states[2])
    if l > 0:
        m['V2'] = P['rk_v2'][l - 1]
        vfc = vf_c_b if d == 0 else vf_c_b[::-1]
        vfx = vf_x_b if d == 0 else vf_x_b[::-1]
        vfin = np.zeros((NT, 128, D), f32)
        for n, (kind, i) in enumerate(chain):
            vfin[n] = (vfx if kind == 'x' else vfc)[i * 128:(i + 1) * 128]
        m['vfin'] = vfin
    return tiles, {k: np.ascontiguousarray(v, dtype=f32) for k, v in m.items()}


def zero_states():
    return [np.zeros((128, 4, 2, 257), np.float32), np.zeros((64, 16, 64), np.float32), np.zeros((128, 8, 128), np.float32)]


def build_combine(kinds, layer, last):
    NT = len(kinds)
    nc = bass.Bass("TRN2", target_bir_lowering=False)
    fw = FW(nc)
    op = fw.op

    def din(name, shape, dt=F32):
        return fw.dram(name, shape, dt, kind="ExternalInput")

    xC = din("xC", [NT, 128, D])
    YM = din("YM", [NT, 2, 128, D])
    YR = din("YR", [NT, 2, 128, D])
    YH = din("YH", [NT, 2, 128, D])
    BON = din("BON", [NT, 128, D])
    WC = din("WC", [D, 7168])
    WO = din("WO", [4, D, D])
    cfm = din("cfm", [128, 8, 2])
    adaw = din("adaw", [D, 3072])
    adab = din("adab", [128, 24])
    normw = din("normw", [128, 8])
    CF = din("CF", [len(CF_NAMES), 128, 128])
    bcin = din("bcin", [1, 5 * 1024])
    biasin = din("biasin", [1, 3072])
    out = fw.dram("out", [NT, 128, D], F32, kind="ExternalOutput")

    cf = fw.sb("cf", [128, len(CF_NAMES), 128], F32)
    cfi = {n: i for i, n in enumerate(CF_NAMES)}
    CFm = lambda n: cf[:, cfi[n], :]
    identb = fw.sb("identb", [128, 128], BF16)
    bc = fw.sb("bc", [128, 5 * 1024], F32)
    gbc = fw.sb("gbc", [128, 2, 1024], F32)
    bias = fw.sb("bias", [33, 3072], BF16)
    ones33 = fw.sb("ones33", [33, 128], BF16)
    modA = fw.sb("modA", [128, 8, 2], F32)
    modB = fw.sb("modB", [128, 8, 2], F32)
    modG = fw.sb("modG", [128, 8, 2], F32)
    wbuf = [fw.sb("wbuf%d" % i, [128, 8, 512], BF16) for i in range(2)]
    hTt = fw.sb("hT", [128, 8, 128], BF16)
    uT = fw.sb("uT", [128, 8, 128], BF16)
    small = fw.sb("small", [128, 64], F32)
    Fs = [fw.sb("F%d" % i, [128, D], F32) for i in range(14)]
    Hs = [fw.sb("H%d" % i, [128, D], BF16) for i in range(6)]
    dg = fw.sb("dg", [128, 128], F32)

    pA = [fw.ps("pA%d" % i, [128, 512], F32) for i in range(2)]
    pT = fw.ps("pT", [128, 1024], BF16)
    pX = [fw.ps("pX%d" % i, [128, 512], F32) for i in range(4)]

    fw.dma("sp", cf[:], CF[:].rearrange("n p f -> p n f"), reads=[CF], writes=[cf])
    op("dve", lambda e: e.tensor_copy(out=identb[:], in_=CFm('ident')), reads=[cf], writes=[identb])
    op("pool", lambda e: e.memset(ones33[:], 1.0), writes=[ones33])
    op("pool", lambda e: e.memset(bias[:], 0.0), writes=[bias])
    fw.dma("sp", bc[0:1, 0:3072], biasin[:], reads=[biasin], writes=[bc])
    fw.dma("sp", bc[32:33, 0:3072], biasin[:], reads=[biasin], writes=[bc])
    op("dve", lambda e: e.tensor_copy(out=bias[0:1, :], in_=bc[0:1, 0:3072]), reads=[bc], writes=[bias])
    op("dve", lambda e: e.tensor_copy(out=bias[32:33, :], in_=bc[32:33, 0:3072]), reads=[bc], writes=[bias])
    op("dve", lambda e: e.tensor_tensor(out=bc[32:33, 0:3072], in0=bc[32:33, 0:3072], in1=bias[32:33, :], op=ALU.subtract),
       reads=[bc, bias], writes=[bc])
    op("dve", lambda e: e.tensor_copy(out=bias[32:33, :], in_=bc[32:33, 0:3072]), reads=[bc], writes=[bias])
    fw.dma("sp", bc[:, :], bcin[0, :].partition_broadcast(128), reads=[bcin], writes=[bc])
    B_MN, B_LW, B_LB, B_HN, B_FN = 0, 1024, 2048, 3072, 4096

    groups = [(WC, None, i * 512) for i in range(14)] + [(WO, j, h * 512) for j in range(4) for h in range(2)]
    gidx = {(id(g[0]), g[1], g[2]): i for i, g in enumerate(groups)}
    wsb = fw.dram("wsb", [128, len(groups), 8, 512], BF16)
    prep_ev = []
    cnt_ = 0
    for gi, (Wd, j, c0) in enumerate(groups):
        for k0 in range(0, 8, 2):
            stg, hb = Fs[cnt_ % 6], Hs[cnt_ % 6]
            eng_ = ("pool", "dve", "act")[cnt_ % 3]
            cnt_ += 1
            sv = stg[:, 0:1024].rearrange("p (k n) -> p k n", k=2)
            hv = hb[:, 0:1024].rearrange("p (k n) -> p k n", k=2)
            srcv = (Wd[k0 * 128:(k0 + 2) * 128, c0:c0 + 512] if j is None else Wd[j, k0 * 128:(k0 + 2) * 128, c0:c0 + 512])
            fw.dma("sp", sv, srcv.rearrange("(k p) n -> p k n", p=128), reads=[Wd], writes=[stg])
            if eng_ == "act":
                op("act", lambda e: e.copy(out=hv, in_=sv), reads=[stg], writes=[hb])
            else:
                op(eng_, lambda e: e.tensor_copy(out=hv, in_=sv), reads=[stg], writes=[hb])
            fw.dma("sp", wsb[:, gi, k0:k0 + 2, :], hv, reads=[hb], writes=[wsb], sem_buf=hb)
            prep_ev.append(wsb.lw)
    for (s_, v_) in prep_ev:
        if fw.waited["sp"].get(id(s_), 0) < v_:
            nc.sync.wait_ge(s_, v_)
            fw.waited["sp"][id(s_)] = v_
            fw.pend["sp"].append((id(s_), v_))
    wcount = [0]

    def load_w(Wd, j, c0):
        wb = wbuf[wcount[0] % 2]
        wcount[0] += 1
        fw.dma("sp", wb[:, :, :], wsb[:, gidx[(id(Wd), j, c0)], :, :], reads=[wsb], writes=[wb])
        return wb

    def mm_proj(ps, lh, wb, bias_off=None):
        for k in range(8):
            op("pe", lambda e, k=k: e.matmul(ps[:, :], lhsT=lh[:, k, :], rhs=wb[:, k, :], start=(k == 0), stop=(k == 7 and bias_off is None)),
               reads=[lh, wb], writes=[ps])
        if bias_off is not None:
            op("pe", lambda e: e.matmul(ps[:, :], lhsT=ones33[:, :], rhs=bias[:, bias_off:bias_off + 512], start=False, stop=True),
               reads=[ones33, bias], writes=[ps])

    emit_modulation(fw, nc, cfm, adaw, adab, normw, modA, modB, modG, wbuf, pX[0], Fs[6], Hs[0], [Fs[7], Fs[8], Fs[9], Fs[10]])
    for s_i in range(2):
        for k in range(8):
            op("dve", lambda e: e.tensor_scalar(out=dg[:], in0=CFm('ident'), scalar1=modG[:, k, s_i:s_i + 1], scalar2=None, op0=ALU.mult),
               reads=[cf, modG], writes=[dg])
            ps_ = pX[1 + k // 4]
            op("pe", lambda e: e.matmul(ps_[:, (k % 4) * 128:(k % 4 + 1) * 128], lhsT=CFm('ones'), rhs=dg[:], start=True, stop=True),
               reads=[cf, dg], writes=[ps_])
        op("act", lambda e: e.copy(out=gbc[:, s_i, 0:512], in_=pX[1][:, :]), reads=[pX[1]], writes=[gbc])
        op("act", lambda e: e.copy(out=gbc[:, s_i, 512:1024], in_=pX[2][:, :]), reads=[pX[2]], writes=[gbc])

    def head_norm(y, H, dh, eps, center, scr):
        yv = y[:].rearrange("p (h c) -> p h c", h=H)
        if center:
            op("dve", lambda e: e.tensor_reduce(out=small[:, 0:H], in_=yv, axis=AX.X, op=ALU.add), reads=[y], writes=[small])
            op("dve", lambda e: e.tensor_scalar(out=small[:, 0:H], in0=small[:, 0:H], scalar1=1.0 / dh, scalar2=None, op0=ALU.mult), reads=[small], writes=[small])
            op("dve", lambda e: e.tensor_tensor(out=yv, in0=yv, in1=small[:, 0:H].unsqueeze(2).broadcast_to([128, H, dh]), op=ALU.subtract),
               reads=[y, small], writes=[y])
        op("act", lambda e: e.activation(out=scr[:], in_=y[:], func=AF.Square), reads=[y], writes=[scr])
        op("dve", lambda e: e.tensor_reduce(out=small[:, 16:16 + H], in_=scr[:].rearrange("p (h c) -> p h c", h=H), axis=AX.X, op=ALU.add),
           reads=[scr], writes=[small])
        op("dve", lambda e: e.tensor_scalar(out=small[:, 16:16 + H], in0=small[:, 16:16 + H], scalar1=1.0 / dh, scalar2=eps, op0=ALU.mult, op1=ALU.add),
           reads=[small], writes=[small])
        op("act", lambda e: e.activation(out=small[:, 16:16 + H], in_=small[:, 16:16 + H], func=AF.Sqrt), reads=[small], writes=[small])
        op("dve", lambda e: e.reciprocal(out=small[:, 32:32 + H], in_=small[:, 16:16 + H]), reads=[small], writes=[small])
        op("dve", lambda e: e.tensor_tensor(out=yv, in0=yv, in1=small[:, 32:32 + H].unsqueeze(2).broadcast_to([128, H, dh]), op=ALU.mult),
           reads=[y, small], writes=[y])

    for ti, kind in enumerate(kinds):
        sidx = 1 if kind == 'c' else 0
        x_ = Fs[13]
        fw.dma("sp", x_[:], xC[ti, :, :], reads=[xC], writes=[x_])
        emit_norm_hT(fw, x_, hTt, modA, modB, sidx, small, Fs[12], CFm('ident'), cf, pX[3], pX[2])
        O, Zm, Zr, Zh = Fs[0], Fs[1], Fs[2], Fs[3]
        G3 = [Fs[4], Fs[5], Fs[6]]
        plan = [(O, AF.Sigmoid, None)] * 2 + [(Zm, AF.Silu, None)] * 2 + [(Zr, AF.Silu, None)] * 2 + [(Zh, AF.Silu, None)] * 2 + \
               [(G3[0], AF.Sigmoid, 0)] * 2 + [(G3[1], AF.Sigmoid, 1024)] * 2 + [(G3[2], AF.Sigmoid, 2048)] * 2
        for gi, (dst, fn, bo) in enumerate(plan):
            wb = load_w(WC, None, gi * 512)
            half = gi % 2
            mm_proj(pA[half], hTt, wb, bias_off=(None if bo is None else bo + half * 512))
            op("act", lambda e: e.activation(out=dst[:, half * 512:half * 512 + 512], in_=pA[half][:, :], func=fn), reads=[pA[half]], writes=[dst])
        op("dve", lambda e: e.tensor_tensor(out=O[:], in0=O[:], in1=Zm[:], op=ALU.mult), reads=[O, Zm], writes=[O])
        merged = Fs[7]
        for bi, (Y, H, dh, eps, center, Z) in enumerate(((YM, 4, 256, EPS, True, O), (YR, 16, 64, RK_GN_EPS, True, Zr), (YH, 8, 128, EPS, False, Zh))):
            ya, yb = Fs[8], Fs[9]
            fw.dma("sp", ya[:], Y[ti, 0, :, :], reads=[Y], writes=[ya])
            fw.dma("sp", yb[:], Y[ti, 1, :, :], reads=[Y], writes=[yb])
            op("dve", lambda e: e.tensor_tensor(out=ya[:], in0=ya[:], in1=yb[:], op=ALU.add), reads=[ya, yb], writes=[ya])
            head_norm(ya, H, dh, eps, center, yb)
            wcol = (B_MN, B_LW, B_HN)[bi]
            op("dve", lambda e: e.tensor_tensor(out=ya[:], in0=ya[:], in1=bc[:, wcol:wcol + 1024], op=ALU.mult), reads=[ya, bc], writes=[ya])
            if bi == 1:
                op("dve", lambda e: e.tensor_tensor(out=ya[:], in0=ya[:], in1=bc[:, B_LB:B_LB + 1024], op=ALU.add), reads=[ya, bc], writes=[ya])
                fw.dma("sp", yb[:], BON[ti, :, :], reads=[BON], writes=[yb])
                op("dve", lambda e: e.tensor_tensor(out=ya[:], in0=ya[:], in1=yb[:], op=ALU.add), reads=[ya, yb], writes=[ya])
            ub = Hs[bi]
            op("dve", lambda e: e.tensor_tensor(out=ub[:], in0=ya[:], in1=Z[:], op=ALU.mult), reads=[ya, Z], writes=[ub])
            for k in range(8):
                op("pe", lambda e, k=k: e.transpose(out=pT[:, k * 128:(k + 1) * 128], in_=ub[:, k * 128:(k + 1) * 128], identity=identb[:]),
                   reads=[ub, identb], writes=[pT])
            op("act", lambda e: e.copy(out=uT[:, :, :], in_=pT[:, :].rearrange("p (g t) -> p g t", g=8)), reads=[pT], writes=[uT])
            for half in range(2):
                wb = load_w(WO, bi, half * 512)
                mm_proj(pA[half], uT, wb)
                sl = slice(half * 512, half * 512 + 512)
                if bi == 0:
                    op("dve", lambda e: e.tensor_tensor(out=merged[:, sl], in0=pA[half][:, :], in1=G3[bi][:, sl], op=ALU.mult), reads=[pA[half], G3[bi]], writes=[merged])
                else:
                    tmpm = Fs[10]
                    op("dve", lambda e: e.tensor_tensor(out=tmpm[:, sl], in0=pA[half][:, :], in1=G3[bi][:, sl], op=ALU.mult), reads=[pA[half], G3[bi]], writes=[tmpm])
                    op("dve", lambda e: e.tensor_tensor(out=merged[:, sl], in0=merged[:, sl], in1=tmpm[:, sl], op=ALU.add), reads=[merged, tmpm], writes=[merged])
        mb = Hs[3]
        op("act", lambda e: e.copy(out=mb[:], in_=merged[:]), reads=[merged], writes=[mb])
        for k in range(8):
            op("pe", lambda e, k=k: e.transpose(out=pT[:, k * 128:(k + 1) * 128], in_=mb[:, k * 128:(k + 1) * 128], identity=identb[:]),
               reads=[mb, identb], writes=[pT])
        op("act", lambda e: e.copy(out=uT[:, :, :], in_=pT[:, :].rearrange("p (g t) -> p g t", g=8)), reads=[pT], writes=[uT])
        res = Fs[11]
        for half in range(2):
            wb = load_w(WO, 3, half * 512)
            mm_proj(pA[half], uT, wb)
            sl = slice(half * 512, half * 512 + 512)
            op("dve", lambda e: e.tensor_tensor(out=res[:, sl], in0=pA[half][:, :], in1=gbc[:, sidx, sl], op=ALU.mult), reads=[pA[half], gbc], writes=[res])
        op("dve", lambda e: e.tensor_tensor(out=res[:], in0=res[:], in1=x_[:], op=ALU.add), reads=[res, x_], writes=[res])
        if last:
            scr = Fs[10]
            op("act", lambda e: e.activation(out=scr[:], in_=res[:], func=AF.Square, accum_out=small[:, 48:49]), reads=[res], writes=[scr, small])
            op("dve", lambda e: e.tensor_scalar(out=small[:, 49:50], in0=small[:, 48:49], scalar1=1.0 / D, scalar2=EPS, op0=ALU.mult, op1=ALU.add), reads=[small], writes=[small])
            op("act", lambda e: e.activation(out=small[:, 49:50], in_=small[:, 49:50], func=AF.Sqrt), reads=[small], writes=[small])
            op("dve", lambda e: e.reciprocal(out=small[:, 50:51], in_=small[:, 49:50]), reads=[small], writes=[small])
            op("dve", lambda e: e.scalar_tensor_tensor(out=res[:], in0=res[:], scalar=small[:, 50:51], in1=bc[:, B_FN:B_FN + 1024], op0=ALU.mult, op1=ALU.mult),
               reads=[res, small, bc], writes=[res])
        fw.dma("sp", out[ti, :, :], res[:], reads=[res], writes=[out], sem_buf=res)
    fw.wait_all("sp", [out])
    fw.close()
    return nc, fw


_PROG_CACHE = {}


def _run(key, builder, in_maps):
    if key not in _PROG_CACHE:
        _PROG_CACHE[key] = builder()[0]
    nc = _PROG_CACHE[key]
    r = run_bass_kernel_spmd(nc, in_maps, core_ids=list(range(len(in_maps))))
    return r.results


def forward(P, NBT, CTX, SEQ, depth=2, nseg=3, cpb=2):
    f32 = np.float32
    CT, XT = CTX // 128, SEQ // 128
    xs = np.array(P['x'], dtype=f32, copy=True)
    cs = np.array(P['ctx'], dtype=f32, copy=True)
    chain = [('c', i) for i in range(CT)] + [('x', i) for i in range(XT)]
    segs = [list(a) for a in np.array_split(np.arange(len(chain)), nseg)]
    segs = [[chain[i] for i in sg] for sg in segs if len(sg)]
    cfc = _const_f32()
    CF = np.stack([cfc[n] for n in CF_NAMES]).astype(f32)
    vf_c = [None] * NBT
    vf_x = [None] * NBT
    ncore = 2 * NBT
    for l in range(depth):
        last = l == depth - 1
        states = [zero_states() for _ in range(ncore)]
        Yc = {k: np.zeros((NBT, 2, CTX, D), f32) for k in ('m', 'r', 'h')}
        Yx = {k: np.zeros((NBT, 2, SEQ, D), f32) for k in ('m', 'r', 'h')}
        bon_c = np.zeros((NBT, CTX, D), f32)
        bon_x = np.zeros((NBT, SEQ, D), f32)
        nvf_c = np.zeros((NBT, CTX, D), f32)
        nvf_x = np.zeros((NBT, SEQ, D), f32)
        for sg in segs:
            in_maps = []
            tiles = None
            for c in range(ncore):
                b, d = c % NBT, c // NBT
                tiles, m = scan_core_inputs(P, l, b, d, cs[b], xs[b], sg, CT, XT, states[c], vf_c[b], vf_x[b])
                in_maps.append(m)
            res = _run(('scan', l, tuple(tiles)), lambda: build_scan(tiles, l), in_maps)
            for c in range(ncore):
                b, d = c % NBT, c // NBT
                o = res[c]
                states[c] = [np.asarray(o['st_ml_out']), np.asarray(o['st_rk_out']), np.asarray(o['st_hg_out'])]
                for n, (kind, i) in enumerate(sg):
                    Ltot = SEQ if kind == 'x' else CTX
                    if d == 0:
                        sl, flip = slice(i * 128, (i + 1) * 128), False
                    else:
                        sl, flip = slice(Ltot - (i + 1) * 128, Ltot - i * 128), True
                    for key, nm in (('m', 'y_m'), ('r', 'y_r'), ('h', 'y_h')):
                        t = np.asarray(o[nm][n])
                        (Yx if kind == 'x' else Yc)[key][b, d, sl] = t[::-1] if flip else t
                    if d == 0:
                        (bon_x if kind == 'x' else bon_c)[b, sl] = np.asarray(o['bon_o'][n])
                        if l == 0:
                            (nvf_x if kind == 'x' else nvf_c)[b, sl] = np.asarray(o['vf_o'][n])
        if l == 0:
            vf_c = [nvf_c[b] for b in range(NBT)]
            vf_x = [nvf_x[b] for b in range(NBT)]
        xpc = XT // cpb
        kinds = ([] if last else ['c']) + ['x'] * xpc
        w_in = P['w_in'][l]
        o = OFF
        WC = np.ascontiguousarray(np.concatenate([w_in[:, o['m_o']:o['m_o'] + 2048], w_in[:, o['r_z']:o['r_z'] + 1024],
                                                  w_in[:, o['h_z']:o['h_z'] + 1024], w_in[:, o['gate']:o['gate'] + 3072]], axis=1), dtype=f32)
        WO = np.ascontiguousarray(np.stack([P['w_pm'][l], P['w_pr'][l], P['w_ph'][l], P['w_out'][l]]), dtype=f32)
        bcin = np.concatenate([P['ml_norm_w'][l], P['rk_ln_w'][l], P['rk_ln_b'][l], P['hg_norm_w'][l], P['final_norm_w']])[None, :].astype(f32)
        biasin = np.ascontiguousarray(P['gate_b'][l].reshape(1, 3072), dtype=f32)
        in_maps = []
        assign = []
        for c in range(NBT * cpb):
            b, part = c // cpb, c % cpb
            tl = ([] if last else [('c', part % CT)]) + [('x', part * xpc + i) for i in range(xpc)]
            assign.append((b, tl))
            NTc = len(tl)
            xC = np.zeros((NTc, 128, D), f32)
            YM = np.zeros((NTc, 2, 128, D), f32)
            YR = np.zeros((NTc, 2, 128, D), f32)
            YH = np.zeros((NTc, 2, 128, D), f32)
            BON = np.zeros((NTc, 128, D), f32)
            for n, (kind, i) in enumerate(tl):
                sl = slice(i * 128, (i + 1) * 128)
                src = xs if kind == 'x' else cs
                Y = Yx if kind == 'x' else Yc
                xC[n] = src[b, sl]
                YM[n] = Y['m'][b, :, sl]
                YR[n] = Y['r'][b, :, sl]
                YH[n] = Y['h'][b, :, sl]
                BON[n] = (bon_x if kind == 'x' else bon_c)[b, sl]
            cfm = np.ascontiguousarray(np.stack([P['c'][b].reshape(8, 128).T, P['c_ctx'].reshape(8, 128).T], axis=-1), dtype=f32)
            in_maps.append(dict(xC=xC, YM=YM, YR=YR, YH=YH, BON=BON, WC=WC, WO=WO, cfm=cfm, adaw=np.ascontiguousarray(P['ada_w'][l], dtype=f32),
                                adab=np.ascontiguousarray(P['ada_b'][l].reshape(24, 128).T, dtype=f32),
                                normw=np.ascontiguousarray(P['norm_w'][l].reshape(8, 128).T, dtype=f32), CF=CF, bcin=bcin, biasin=biasin))
        res = _run(('comb', l, tuple(kinds), last), lambda: build_combine(kinds, l, last), in_maps)
        for c, (b, tl) in enumerate(assign):
            oc = np.asarray(res[c]['out'])
            for n, (kind, i) in enumerate(tl):
                (xs if kind == 'x' else cs)[b, i * 128:(i + 1) * 128] = oc[n]
    return xs


def kernel(**inputs):
    P = {k: np.asarray(v) for k, v in inputs.items()}
    return forward(P, 4, 256, 8192).astype(np.float32)
```
